# Optimizing a Trainium2 kernel written in Bass

```python
import math
import jax, jax.numpy as jnp
from jax import lax
import numpy as np

D_MODEL = 4096
BATCH = 2
SEQ = 8192
DEPTH = 2

CONV_CH = D_MODEL // 2
CONV_HEADS = 16
CONV_K = 3
HG_HEADS = 16
HG_DK = (D_MODEL // 2) // HG_HEADS
HG_DV = HG_DK
HG_WIDTH = HG_HEADS * HG_DK
HG_CHUNK = 32
EVEN_MIX = CONV_CH + HG_WIDTH
EVEN_IN = 3 * CONV_CH + 3 * HG_WIDTH + EVEN_MIX
S5_WIDTH = D_MODEL
S5_GROUP = 16
S5_GROUPS = S5_WIDTH // S5_GROUP
S5_STATE = 64
S5_CHUNK = 128
ODD_IN = 2 * S5_WIDTH
N_EVEN = (DEPTH + 1) // 2
N_ODD = DEPTH // 2
ALPHA = (2 * DEPTH) ** 0.25
BETA = (8 * DEPTH) ** -0.25
LN_EPS = 1e-5
RMS_EPS = 1e-6
LAMBDA_RE_MAX = -1e-4

kernel_name = "hybrid_shortconv_hgrn2_s5_deepnorm"


def layer_norm(x, g, b):
    x32 = x.astype(jnp.float32)
    mu = jnp.mean(x32, axis=-1, keepdims=True)
    xc = x32 - mu
    var = jnp.mean(xc * xc, axis=-1, keepdims=True)
    return (xc * lax.rsqrt(var + LN_EPS) * g.astype(jnp.float32) + b.astype(jnp.float32)).astype(x.dtype)


def causal_short_conv(u, w):
    K = w.shape[0]
    S = u.shape[1]
    up = jnp.pad(u, ((0, 0), (K - 1, 0), (0, 0)))
    return sum(up[:, k:k + S, :] * w[k] for k in range(K))


def hgrn2_chunkwise(q, f_pre, v, lb):
    Bsz, S, _ = q.shape
    N = S // HG_CHUNK
    f = lb + (1.0 - lb) * jax.nn.sigmoid(f_pre)
    log_f = jnp.log(f)
    k = 1.0 - f

    def heads(t, d):
        return t.reshape(Bsz, N, HG_CHUNK, HG_HEADS, d).transpose(0, 3, 1, 2, 4)

    q, k, log_f, v = heads(q, HG_DK), heads(k, HG_DK), heads(log_f, HG_DK), heads(v, HG_DV)
    b = jnp.cumsum(log_f, axis=3)
    b_last = b[..., -1:, :]
    q_in = q * jnp.exp(b)
    k_in = k * jnp.exp(-b)
    k_dec = k * jnp.exp(b_last - b)
    chunk_decay = jnp.exp(b_last[..., 0, :])

    causal = jnp.tril(jnp.ones((HG_CHUNK, HG_CHUNK), dtype=bool))
    scores = jnp.einsum('bhncd,bhnsd->bhncs', q_in, k_in)
    scores = jnp.where(causal, scores, 0.0)
    o_intra = jnp.einsum('bhncs,bhnsv->bhncv', scores, v)

    def step(state, xs):
        q_c, kd_c, v_c, dec_c = xs
        o_c = jnp.einsum('bhcd,bhdv->bhcv', q_c, state)
        state = dec_c[..., None] * state + jnp.einsum('bhcd,bhcv->bhdv', kd_c, v_c)
        return state, o_c

    xs = (jnp.moveaxis(q_in, 2, 0), jnp.moveaxis(k_dec, 2, 0),
          jnp.moveaxis(v, 2, 0), jnp.moveaxis(chunk_decay, 2, 0))
    state0 = jnp.zeros((Bsz, HG_HEADS, HG_DK, HG_DV), jnp.float32)
    _, o_inter = lax.scan(step, state0, xs)
    o = o_intra + jnp.moveaxis(o_inter, 0, 2)
    return o.transpose(0, 2, 3, 1, 4).reshape(Bsz, S, HG_HEADS, HG_DV)


def even_mixer(x, w_in, conv_w, hg_norm, lb, w_out):
    Bsz, S, _ = x.shape
    proj = jnp.einsum('bsd,de->bse', x, w_in).astype(jnp.float32)
    c1, c2, c3 = CONV_CH, 2 * CONV_CH, 3 * CONV_CH
    h1, h2, h3 = c3 + HG_WIDTH, c3 + 2 * HG_WIDTH, c3 + 3 * HG_WIDTH
    a_b, a_c, a_h = proj[..., :c1], proj[..., c1:c2], proj[..., c2:c3]
    q, f_pre, v = proj[..., c3:h1], proj[..., h1:h2], proj[..., h2:h3]
    gate = proj[..., h3:]
    y_a = a_b * causal_short_conv(a_c * a_h, conv_w.astype(jnp.float32))
    o = hgrn2_chunkwise(q, f_pre, v, lb)
    o = o * lax.rsqrt(jnp.mean(o * o, axis=-1, keepdims=True) + RMS_EPS) * hg_norm.astype(jnp.float32)
    y_b = o.reshape(Bsz, S, HG_WIDTH)
    y = jnp.concatenate([y_a, y_b], axis=-1) * jax.nn.silu(gate)
    return jnp.einsum('bse,ed->bsd', y.astype(x.dtype), w_out)


def s5_binop(e1, e2):
    a1, b1 = e1
    a2, b2 = e2
    return a1 * a2, a2 * b1 + b2


def odd_mixer(x, w_in, lam_re, lam_im, log_step, b_re, b_im, c_re, c_im, d_skip, w_glu, b_glu, w_out):
    Bsz, S, _ = x.shape
    N = S // S5_CHUNK
    proj = jnp.einsum('bsd,de->bse', x, w_in).astype(jnp.float32)
    u, gate = proj[..., :S5_WIDTH], proj[..., S5_WIDTH:]
    lam = lax.complex(jnp.minimum(lam_re.astype(jnp.float32), LAMBDA_RE_MAX), lam_im.astype(jnp.float32))
    dt = jnp.exp(log_step.astype(jnp.float32))[:, None]
    lam_dt = lam * dt
    lam_bar = jnp.exp(lam_dt)
    b_bar = ((lam_bar - 1.0) / lam)[..., None] * lax.complex(b_re.astype(jnp.float32), b_im.astype(jnp.float32))
    b_bar_re, b_bar_im = jnp.real(b_bar), jnp.imag(b_bar)
    c_re32, c_im32 = c_re.astype(jnp.float32), c_im.astype(jnp.float32)
    lam_pow = jnp.exp(jnp.arange(1, S5_CHUNK + 1, dtype=jnp.float32)[:, None, None] * lam_dt)
    u_chunks = u.reshape(Bsz, N, S5_CHUNK, S5_GROUPS, S5_GROUP).transpose(1, 0, 2, 3, 4)

    def chunk_step(state, u_c):
        bu = lax.complex(jnp.einsum('blgc,gpc->blgp', u_c, b_bar_re),
                         jnp.einsum('blgc,gpc->blgp', u_c, b_bar_im))
        a = jnp.broadcast_to(lam_bar, bu.shape)
        _, xs = lax.associative_scan(s5_binop, (a, bu), axis=1)
        xs = xs + lam_pow[None] * state[:, None]
        y = (jnp.einsum('blgp,gcp->blgc', jnp.real(xs), c_re32)
             - jnp.einsum('blgp,gcp->blgc', jnp.imag(xs), c_im32))
        return xs[:, -1], y

    state0 = jnp.zeros((Bsz, S5_GROUPS, S5_STATE), jnp.complex64)
    _, y = lax.scan(chunk_step, state0, u_chunks)
    y = y.transpose(1, 0, 2, 3, 4).reshape(Bsz, S, S5_WIDTH)
    y = jax.nn.gelu(y + d_skip.astype(jnp.float32) * u)
    y = y * jax.nn.sigmoid(jnp.einsum('bse,ef->bsf', y, w_glu.astype(jnp.float32)) + b_glu.astype(jnp.float32))
    y = y * jax.nn.silu(gate)
    return jnp.einsum('bse,ed->bsd', y.astype(x.dtype), w_out)


def setup_inputs(seed: int = 0) -> dict:
    key = jax.random.key(seed)
    ks = jax.random.split(key, 24)
    f32 = jnp.float32
    nrm = lambda k, shape, s: jax.random.normal(k, shape, f32) * s
    x = jax.random.normal(ks[0], (BATCH, SEQ, D_MODEL), f32)
    ev_w_in = nrm(ks[1], (N_EVEN, D_MODEL, EVEN_IN), D_MODEL ** -0.5)
    ev_conv_w = nrm(ks[2], (N_EVEN, CONV_K, CONV_CH), CONV_K ** -0.5)
    ev_hg_norm = 1.0 + nrm(ks[3], (N_EVEN, HG_DV), 0.02)
    ev_w_out = nrm(ks[4], (N_EVEN, EVEN_MIX, D_MODEL), BETA * EVEN_MIX ** -0.5)
    ev_ln_g = 1.0 + nrm(ks[5], (N_EVEN, D_MODEL), 0.02)
    ev_ln_b = nrm(ks[6], (N_EVEN, D_MODEL), 0.02)
    hg_lb_logits = nrm(ks[7], (DEPTH + 1, HG_WIDTH), 0.1)
    od_w_in = nrm(ks[8], (N_ODD, D_MODEL, ODD_IN), D_MODEL ** -0.5)
    od_lam_re = -0.5 + nrm(ks[9], (N_ODD, S5_GROUPS, S5_STATE), 0.01)
    od_lam_im = (math.pi * jnp.arange(S5_STATE, dtype=f32))[None, None, :] + nrm(ks[10], (N_ODD, S5_GROUPS, S5_STATE), 0.01)
    od_log_step = jax.random.uniform(ks[11], (N_ODD, S5_GROUPS), f32, math.log(1e-3), math.log(1e-1))
    od_b_re = nrm(ks[12], (N_ODD, S5_GROUPS, S5_STATE, S5_GROUP), (2 * S5_GROUP) ** -0.5)
    od_b_im = nrm(ks[13], (N_ODD, S5_GROUPS, S5_STATE, S5_GROUP), (2 * S5_GROUP) ** -0.5)
    od_c_re = nrm(ks[14], (N_ODD, S5_GROUPS, S5_GROUP, S5_STATE), S5_STATE ** -0.5)
    od_c_im = nrm(ks[15], (N_ODD, S5_GROUPS, S5_GROUP, S5_STATE), S5_STATE ** -0.5)
    od_d = nrm(ks[16], (N_ODD, S5_WIDTH), 1.0)
    od_w_glu = nrm(ks[17], (N_ODD, S5_WIDTH, S5_WIDTH), S5_WIDTH ** -0.5)
    od_b_glu = nrm(ks[18], (N_ODD, S5_WIDTH), 0.02)
    od_w_out = nrm(ks[19], (N_ODD, S5_WIDTH, D_MODEL), BETA * S5_WIDTH ** -0.5)
    od_ln_g = 1.0 + nrm(ks[20], (N_ODD, D_MODEL), 0.02)
    od_ln_b = nrm(ks[21], (N_ODD, D_MODEL), 0.02)
    return {"x": x, "ev_w_in": ev_w_in, "ev_conv_w": ev_conv_w, "ev_hg_norm": ev_hg_norm,
            "ev_w_out": ev_w_out, "ev_ln_g": ev_ln_g, "ev_ln_b": ev_ln_b, "hg_lb_logits": hg_lb_logits,
            "od_w_in": od_w_in, "od_lam_re": od_lam_re, "od_lam_im": od_lam_im, "od_log_step": od_log_step,
            "od_b_re": od_b_re, "od_b_im": od_b_im, "od_c_re": od_c_re, "od_c_im": od_c_im,
            "od_d": od_d, "od_w_glu": od_w_glu, "od_b_glu": od_b_glu, "od_w_out": od_w_out,
            "od_ln_g": od_ln_g, "od_ln_b": od_ln_b}


def reference(x, ev_w_in, ev_conv_w, ev_hg_norm, ev_w_out, ev_ln_g, ev_ln_b, hg_lb_logits,
              od_w_in, od_lam_re, od_lam_im, od_log_step, od_b_re, od_b_im, od_c_re, od_c_im,
              od_d, od_w_glu, od_b_glu, od_w_out, od_ln_g, od_ln_b):
    lb_all = jnp.cumsum(jax.nn.softmax(hg_lb_logits.astype(jnp.float32), axis=0), axis=0)
    h = x
    for layer in range(DEPTH):
        j = layer // 2
        if layer % 2 == 0:
            y = even_mixer(h, ev_w_in[j], ev_conv_w[j], ev_hg_norm[j], lb_all[layer], ev_w_out[j])
            h = layer_norm(ALPHA * h + y, ev_ln_g[j], ev_ln_b[j])
        else:
            y = odd_mixer(h, od_w_in[j], od_lam_re[j], od_lam_im[j], od_log_step[j], od_b_re[j], od_b_im[j],
                          od_c_re[j], od_c_im[j], od_d[j], od_w_glu[j], od_b_glu[j], od_w_out[j])
            h = layer_norm(ALPHA * h + y, od_ln_g[j], od_ln_b[j])
    return h
```

```python
from contextlib import ExitStack
import numpy as np
import concourse.bass as bass
import concourse.mybir as mybir
from concourse.bass_utils import run_bass_kernel_spmd

F32 = mybir.dt.float32
F32R = mybir.dt.float32r
AF = mybir.ActivationFunctionType
ALU = mybir.AluOpType
AX = mybir.AxisListType

D = 4096
KT = 32
TB = 512
NCORES = 8
ALPHA = 4.0 ** 0.25
LN_EPS = 1e-5
RMS_EPS = 1e-6


class Buf:
    def __init__(self, name):
        self.name = name
        self.w = None
        self.r = {}
        self.dsem = None
        self.dcnt = 0


class T:
    def __init__(self, t, name):
        self.t = t
        self.b = Buf(name)

    def __getitem__(self, k):
        return self.t[k]


class Prog:
    def __init__(self, nc):
        self.nc = nc
        self.eng = {"pe": nc.tensor, "act": nc.scalar, "dve": nc.vector, "pool": nc.gpsimd, "sp": nc.sync}
        self.sem = {e: nc.alloc_semaphore("s_" + e) for e in ["pe", "act", "dve", "pool"]}
        self.cnt = {e: 0 for e in self.sem}
        self.waited = {}
        self.dsems = {}
        self.all_dsems = []
        self.ninst = 0

    def tile(self, es, name, shape, dtype=F32):
        self.uid = getattr(self, "uid", 0) + 1
        t = es.enter_context(self.nc.sbuf_tensor(f"{name}_{self.uid}", shape, dtype))
        return T(t, name)

    def ptile(self, es, name, shape, dtype=F32):
        self.uid = getattr(self, "uid", 0) + 1
        t = es.enter_context(self.nc.psum_tensor(f"{name}_{self.uid}", shape, dtype))
        return T(t, name)

    def dram(self, name, shape, dtype=F32, kind="Internal"):
        t = self.nc.dram_tensor(name, shape, dtype, kind=kind)
        return T(t.ap(), name)

    def _wait(self, e, tok):
        sem, val = tok
        key = (e, sem.num)
        if self.waited.get(key, 0) >= val:
            return
        self.waited[key] = val
        self.eng[e].wait_ge(sem, val)

    def _deps(self, e, reads, writes, nosame=False):
        toks = []
        for b in reads:
            if b.w:
                toks.append(b.w)
        for b in writes:
            if b.w:
                toks.append(b.w)
            toks.extend(b.r.values())
        for t in toks:
            if t[0] is self.sem.get(e) and (e == "pe" or nosame):
                continue
            self._wait(e, t)

    def _post(self, tok, reads, writes):
        for b in writes:
            b.w = tok
            b.r = {}
        for b in reads:
            if b not in writes:
                b.r[tok[0].num] = tok

    def op(self, e, fn, reads=(), writes=(), nosame=False):
        reads = [x.b if isinstance(x, T) else x for x in reads]
        writes = [x.b if isinstance(x, T) else x for x in writes]
        self._deps(e, reads, writes, nosame)
        inst = fn(self.eng[e])
        self.cnt[e] += 1
        self.ninst += 1
        inst.then_inc(self.sem[e], 1)
        tok = (self.sem[e], self.cnt[e])
        self._post(tok, reads, writes)

    def dma(self, e, out_ap, in_ap, reads=(), writes=()):
        reads = [x.b if isinstance(x, T) else x for x in reads]
        writes = [x.b if isinstance(x, T) else x for x in writes]
        self._deps(e, reads, writes)
        sb = writes[0]
        if sb.dsem is None:
            if sb.name not in self.dsems:
                self.dsems[sb.name] = [self.nc.alloc_semaphore("d_" + sb.name), 0]
                self.all_dsems.append(self.dsems[sb.name])
            sb.dsem = self.dsems[sb.name]
        sb.dsem[1] += 16
        self.eng[e].dma_start(out=out_ap, in_=in_ap).then_inc(sb.dsem[0], 16)
        self.ninst += 1
        tok = (sb.dsem[0], sb.dsem[1])
        self._post(tok, reads, writes)

    def barrier(self):
        for e in ["pe", "act", "dve", "pool", "sp"]:
            for e2 in self.sem:
                if e2 != e and self.cnt[e2] > 0:
                    self._wait(e, (self.sem[e2], self.cnt[e2]))
            for ds in self.all_dsems:
                if ds[1] > 0:
                    self._wait(e, (ds[0], ds[1]))


def build(NT, mode):
    NB = NT // TB
    nc = bass.Bass("TRN2", target_bir_lowering=False)
    nc.dge_precook = False
    P = Prog(nc)
    din = lambda name, shape, dt=F32: T(nc.dram_tensor(name, shape, dt, kind="ExternalInput").ap(), name)
    dout = lambda name, shape, dt=F32: T(nc.dram_tensor(name, shape, dt, kind="ExternalOutput").ap(), name)

    LOGNT = NT.bit_length() - 1
    prm = din("prm", [3, 128, 128], F32)
    if mode in ("A", "B"):
        xT_in = din("xT", [128, KT, NT], F32R)
    if mode == "A":
        w0 = din("w0", [32, 128, KT, 128], F32R)
        sloc_out = dout("sloc", [128, 16, 128], F32)
        dtot_out = dout("dtot", [128, 16], F32)
        WMAP = lambda c: c - 64
    if mode == "B":
        xhT_in = din("xhT", [128, KT, 2], F32)
        w0 = din("w0", [128, 128, KT, 128], F32R)
        wo0 = din("wo0", [32, 128, KT, 128], F32R)
        sl_S = din("sl_S", [3, 128, 16, 128], F32)
        sl_D = din("sl_D", [3, 128, 16], F32)
        h1T_out = dout("h1T", [128, KT, NT], F32R)
        xloc_out = dout("xloc", [2, 128, 128], F32)
        WMAP = lambda c: c
    if mode == "C":
        h1T_in = din("h1T", [128, KT, NT], F32R)
        wg1 = din("wg1", [32, 128, KT, 128], F32R)
        wo1 = din("wo1", [32, 128, KT, 128], F32R)
        xs_in = din("xs", [3, 2, 128, 128], F32)
        outT = dout("outT", [128, KT, NT], F32)
    if mode in ("B", "C"):
        w1 = din("w1", [64 if mode == "C" else 32, 128, KT, 128], F32R)
        lamT = din("lamT", [3, 128, 128], F32)
        Bh = din("Bh", [2, 128, 32, 128], F32)
        Ch = din("Ch", [2, 128, 128, 32], F32)

    yT_d = P.dram("yT_d", [32, 128, TB], F32R)
    hp_d = P.dram("hp_d", [32, 128, TB], F32)
    sg_d = P.dram("sg_d", [32, 128, TB], F32)
    y2_d = P.dram("y2_d", [32, 128, TB], F32R)

    with ExitStack() as g:
        ident = P.tile(g, "ident", [128, 128])
        ones = P.tile(g, "ones", [128, 128])
        cmask = P.tile(g, "cmask", [128, TB])
        maskT = P.tile(g, "maskT", [128, 128])
        prmT = P.tile(g, "prmT", [128, 3, 128])
        prm_s = P.tile(g, "prm_s", [128, 3, 128])
        lbc = P.tile(g, "lbc", [128, 16])
        omlb = P.tile(g, "omlb", [128, 16])
        SD = 128 if mode != "C" else 1
        S = [[P.tile(g, f"S{h}_{i}", [128, SD]) for i in range(2)] for h in range(16)]
        Spp = [0] * 16
        zhalo = P.tile(g, "zhalo", [128, 16, 2])
        lsum = P.tile(g, "lsum", [128, 16])

        P.op("pool", lambda e: e.memset(ident[:], 0.0), writes=[ident])
        P.op("pool", lambda e: e.affine_select(out=ident[:], in_=ident[:], pattern=[[-1, 128]], compare_op=ALU.not_equal,
                                               fill=1.0, base=0, channel_multiplier=1), reads=[ident], writes=[ident])
        P.op("pool", lambda e: e.memset(ones[:], 1.0), writes=[ones])
        P.op("pool", lambda e: e.memset(cmask[:], 1.0), writes=[cmask])
        P.op("pool", lambda e: e.memset(cmask[:].rearrange("p (n c) -> p n c", c=32)[:, :, 0:1], 0.0), reads=[cmask], writes=[cmask])
        P.op("pool", lambda e: e.memset(maskT[:], 1.0), writes=[maskT])
        P.op("pool", lambda e: e.affine_select(out=maskT[:], in_=maskT[:], pattern=[[1, 128]], compare_op=ALU.is_ge,
                                               fill=0.0, base=0, channel_multiplier=-1), reads=[maskT], writes=[maskT])
        for i in range(3):
            P.op("pool", lambda e, i=i: e.memset(maskT[32 * i:32 * i + 32, 32 * (i + 1):128], 0.0), reads=[maskT], writes=[maskT])
        P.op("pool", lambda e: e.memset(lsum[:], 0.0), writes=[lsum])

        P.dma("sp", prm_s[:], prm[:].rearrange("a r c -> r a c"), reads=[prm], writes=[prm_s])
        with ExitStack() as es:
            pt = P.ptile(es, "pt0", [128, 3, 128])
            for a in range(3):
                P.op("pe", lambda e, a=a: e.transpose(pt[:, a, :], prm_s[:, a, :], ident[:]), reads=[prm_s, ident], writes=[pt])
            P.op("dve", lambda e: e.tensor_copy(prmT[:], pt[:]), reads=[pt], writes=[prmT])
            ex = P.tile(es, "ex", [128, 48])
            sm = P.tile(es, "sm", [128, 16])
            P.op("act", lambda e: e.activation(out=ex[:], in_=prmT[:, 0, 0:48], func=AF.Exp), reads=[prmT], writes=[ex])
            P.op("dve", lambda e: e.tensor_tensor(out=sm[:], in0=ex[:, 0:16], in1=ex[:, 16:32], op=ALU.add), reads=[ex], writes=[sm])
            P.op("dve", lambda e: e.tensor_tensor(out=sm[:], in0=sm[:], in1=ex[:, 32:48], op=ALU.add), reads=[ex, sm], writes=[sm])
            P.op("dve", lambda e: e.reciprocal(sm[:], sm[:]), reads=[sm], writes=[sm])
            P.op("dve", lambda e: e.tensor_tensor(out=lbc[:], in0=ex[:, 0:16], in1=sm[:], op=ALU.mult), reads=[ex, sm], writes=[lbc])
            P.op("dve", lambda e: e.tensor_scalar(omlb[:], lbc[:], -1.0, 1.0, ALU.mult, ALU.add), reads=[lbc], writes=[omlb])
            P.barrier()
        convw = lambda k, t: prmT[:, 0, 48 + k * 16 + t:48 + k * 16 + t + 1]
        lng0 = lambda j: prmT[:, 0, 96 + j:97 + j]
        lnb0 = lambda j: prmT[:, 1, j:j + 1]
        hgn = prmT[:, 2, 32:33]

        if mode == "A":
            for h in range(16):
                P.op("pool", lambda e, h=h: e.memset(S[h][0][:], 0.0), writes=[S[h][0]])
        if mode == "B":
            with ExitStack() as es:
                sacc = P.tile(es, "sacc", [128, 16, 128])
                sld = P.tile(es, "sld", [128, 16, 128])
                dsl = P.tile(es, "dsl", [128, 3, 16])
                P.dma("sp", dsl[:], sl_D[:].rearrange("a p h -> p a h"), reads=[sl_D], writes=[dsl])
                P.op("act", lambda e: e.activation(out=dsl[:], in_=dsl[:], func=AF.Exp), reads=[dsl], writes=[dsl])
                P.op("pool", lambda e: e.memset(sacc[:], 0.0), writes=[sacc])
                for a in range(3):
                    P.dma("sp", sld[:], sl_S[a], reads=[sl_S], writes=[sld])
                    P.op("dve", lambda e, a=a: e.tensor_tensor(out=sacc[:], in0=sacc[:], in1=dsl[:, a, :].unsqueeze(2).broadcast_to([128, 16, 128]), op=ALU.mult),
                         reads=[sacc, dsl], writes=[sacc])
                    P.op("dve", lambda e: e.tensor_tensor(out=sacc[:], in0=sacc[:], in1=sld[:], op=ALU.add), reads=[sacc, sld], writes=[sacc])
                for h in range(16):
                    P.op("pool", lambda e, h=h: e.tensor_copy(S[h][0][:], sacc[:, h, :]), reads=[sacc], writes=[S[h][0]])
                P.barrier()

        def phase_out(t0, y_d, wo, res_ap, lng, lnb, out_ap, odt):
            with ExitStack() as es:
                yT = P.tile(es, "yT", [128, KT, TB], F32R)
                wb = [P.tile(es, f"wbB{i}", [128, KT, 128], F32R) for i in range(3)]
                xr = [P.tile(es, f"xr{i}", [128, TB], F32R) for i in range(2)]
                hp = [P.tile(es, f"hp{i}", [128, TB]) for i in range(2)]
                sq = [P.tile(es, f"sq{i}", [128, TB]) for i in range(2)]
                ho = [P.tile(es, f"ho{i}", [128, TB], odt) for i in range(2)]
                mean = P.tile(es, "mean", [128, TB])
                rstd = P.tile(es, "rstd", [128, TB])
                nmr = P.tile(es, "nmr", [128, TB])
                pA = [P.ptile(es, f"pA{i}", [128, TB]) for i in range(2)]
                pSum = P.ptile(es, "pSum", [128, TB])
                pSq = P.ptile(es, "pSq", [128, TB])
                for k4 in range(4):
                    P.dma("sp", yT[:, 8 * k4:8 * k4 + 8, :], y_d[8 * k4:8 * k4 + 8].rearrange("k p t -> p k t"), reads=[], writes=[yT])
                for j in range(KT):
                    w = wb[j % 3]
                    P.dma("sp", w[:], wo[j], reads=[], writes=[w])
                    x_ = xr[j % 2]
                    P.dma("sp", x_[:], res_ap(j), reads=[], writes=[x_])
                    ps = pA[j % 2]
                    for k in range(KT):
                        P.op("pe", lambda e, k=k: e.matmul(ps[:], w[:, k, :], yT[:, k, :], start=(k == 0), stop=(k == KT - 1)), reads=[w, yT], writes=[ps])
                    h_ = hp[j % 2]
                    P.op("dve", lambda e: e.scalar_tensor_tensor(out=h_[:], in0=x_[:].bitcast(F32), scalar=float(ALPHA), in1=ps[:], op0=ALU.mult, op1=ALU.add),
                         reads=[x_, ps], writes=[h_])
                    s_ = sq[j % 2]
                    P.op("act", lambda e: e.activation(out=s_[:], in_=h_[:], func=AF.Square), reads=[h_], writes=[s_])
                    P.op("pe", lambda e: e.matmul(pSum[:], ones[:], h_[:], start=(j == 0), stop=(j == KT - 1)), reads=[ones, h_], writes=[pSum])
                    P.op("pe", lambda e: e.matmul(pSq[:], ones[:], s_[:], start=(j == 0), stop=(j == KT - 1)), reads=[ones, s_], writes=[pSq])
                    P.dma("act", hp_d[j], h_[:], reads=[h_], writes=[Buf("hp_d")])
                P.barrier()
                P.op("act", lambda e: e.mul(mean[:], pSum[:], 1.0 / D), reads=[pSum], writes=[mean])
                P.op("dve", lambda e: e.tensor_tensor(out=nmr[:], in0=mean[:], in1=mean[:], op=ALU.mult), reads=[mean], writes=[nmr])
                P.op("dve", lambda e: e.scalar_tensor_tensor(out=rstd[:], in0=pSq[:], scalar=1.0 / D, in1=nmr[:], op0=ALU.mult, op1=ALU.subtract), reads=[pSq, nmr], writes=[rstd])
                P.op("act", lambda e: e.activation(out=rstd[:], in_=rstd[:], func=AF.Ln, bias=epsl[:, 0:1]), reads=[rstd, epsl], writes=[rstd])
                P.op("act", lambda e: e.activation(out=rstd[:], in_=rstd[:], func=AF.Exp, scale=-0.5), reads=[rstd], writes=[rstd])
                P.op("dve", lambda e: e.scalar_tensor_tensor(out=nmr[:], in0=mean[:], scalar=-1.0, in1=rstd[:], op0=ALU.mult, op1=ALU.mult), reads=[mean, rstd], writes=[nmr])
                for j in range(KT):
                    h_ = hp[j % 2]
                    P.dma("sp", h_[:], hp_d[j], reads=[], writes=[h_])
                    s_ = sq[j % 2]
                    P.op("dve", lambda e: e.tensor_tensor(out=s_[:], in0=h_[:], in1=rstd[:], op=ALU.mult), reads=[h_, rstd], writes=[s_])
                    P.op("pool", lambda e: e.tensor_tensor(out=s_[:], in0=s_[:], in1=nmr[:], op=ALU.add), reads=[s_, nmr], writes=[s_])
                    o_ = ho[j % 2]
                    P.op("dve", lambda e: e.tensor_scalar(o_[:], s_[:], lng(j), lnb(j), ALU.mult, ALU.add), reads=[s_, prmT], writes=[o_])
                    P.dma("pool", out_ap(j), o_[:], reads=[o_], writes=[Buf("h1T")])
                P.barrier()


        def l0_block(blk, full):
            t0 = blk * TB
            with ExitStack() as es:
                xT = P.tile(es, "xT", [128, KT, TB], F32R)
                wb = [P.tile(es, f"wb{i}", [128, KT, 128], F32R) for i in range(3)]
                wi = [0]
                tmp = [P.tile(es, f"tmp{i}", [128, TB]) for i in range(14)]
                qraw = P.tile(es, "qraw", [128, TB])
                czc = P.tile(es, "czc", [128, TB])
                cab = P.tile(es, "cab", [128, TB])
                csg = P.tile(es, "csg", [128, TB])
                hf2 = [P.tile(es, f"hf{i}", [128, TB]) for i in range(2)]
                hv2 = [P.tile(es, f"hv{i}", [128, TB]) for i in range(2)]
                hsg = P.tile(es, "hsg", [128, TB])
                zbuf = P.tile(es, "zbuf", [128, TB + 2])
                ytile = [P.tile(es, f"yt{i}", [128, TB], F32R) for i in range(2)]
                yi = [0]
                kv = P.tile(es, "kv", [128, 4, 256])
                AT = P.tile(es, "AT", [128, 4, 128])
                xh = P.tile(es, "xh", [128, KT, 2])
                pp = [P.ptile(es, f"pp{i}", [128, TB]) for i in range(4)]
                pT = P.ptile(es, "pT", [128, 2, 256])
                pS = P.ptile(es, "pS", [128, 4, 128])
                pO = P.ptile(es, "pO", [128, TB])
                pU = P.ptile(es, "pU", [128, 128])

                for k4 in range(4):
                    P.dma("sp", xT[:, 8 * k4:8 * k4 + 8, :], xT_in[:, 8 * k4:8 * k4 + 8, t0:t0 + TB], reads=[xT_in], writes=[xT])
                if blk == 0 and full:
                    P.dma("sp", xh[:], xhT_in[:], reads=[xhT_in], writes=[xh])

                def loadw(chunk):
                    w = wb[wi[0] % 3]
                    wi[0] += 1
                    P.dma("sp", w[:], w0[WMAP(chunk)], reads=[w0], writes=[w])
                    return w

                def proj(w, ps):
                    for k in range(KT):
                        P.op("pe", lambda e, k=k: e.matmul(ps[:], w[:, k, :], xT[:, k, :], start=(k == 0), stop=(k == KT - 1)),
                             reads=[w, xT], writes=[ps])

                def store_y(ysb, etile):
                    P.dma("pool", yT_d[etile], ysb[:], reads=[ysb], writes=[Buf("yT_d")])

                def conv_tile(i):
                    wc = loadw(16 + i)
                    proj(wc, pp[0])
                    if blk == 0:
                        for k in range(KT):
                            P.op("pe", lambda e, k=k: e.matmul(pU[:, 0:2], wc[:, k, :].bitcast(F32), xh[:, k, :], start=(k == 0), stop=(k == KT - 1)),
                                 reads=[wc, xh], writes=[pU])
                    wh = loadw(32 + i)
                    proj(wh, pp[1])
                    if blk == 0:
                        for k in range(KT):
                            P.op("pe", lambda e, k=k: e.matmul(pU[:, 2:4], wh[:, k, :].bitcast(F32), xh[:, k, :], start=(k == 0), stop=(k == KT - 1)),
                                 reads=[wh, xh], writes=[pU])
                    wbb = loadw(0 + i)
                    proj(wbb, pp[2])
                    wg = loadw(96 + i)
                    proj(wg, pp[3])
                    zc, c1, c2, ya, sg, abs_ = czc, tmp[1], tmp[2], tmp[3], csg, cab
                    P.op("act", lambda e: e.copy(zc[:], pp[0][:]), reads=[pp[0]], writes=[zc])
                    P.op("dve", lambda e: e.tensor_tensor(out=zbuf[:, 2:TB + 2], in0=zc[:], in1=pp[1][:], op=ALU.mult), reads=[zc, pp[1]], writes=[zbuf])
                    P.op("act", lambda e: e.copy(abs_[:], pp[2][:]), reads=[pp[2]], writes=[abs_])
                    P.op("act", lambda e: e.activation(out=sg[:], in_=pp[3][:], func=AF.Silu), reads=[pp[3]], writes=[sg])
                    if blk == 0:
                        P.op("act", lambda e: e.copy(zc[:, 0:2], pU[:, 0:2]), reads=[pU], writes=[zc])
                        P.op("dve", lambda e: e.tensor_tensor(out=zbuf[:, 0:2], in0=zc[:, 0:2], in1=pU[:, 2:4], op=ALU.mult), reads=[zc, pU, zbuf], writes=[zbuf])
                    else:
                        P.op("pool", lambda e: e.tensor_copy(zbuf[:, 0:2], zhalo[:, i, :]), reads=[zhalo, zbuf], writes=[zbuf])
                    P.op("pool", lambda e: e.tensor_copy(zhalo[:, i, :], zbuf[:, TB:TB + 2]), reads=[zbuf, zhalo], writes=[zhalo])
                    yield
                    P.op("pool", lambda e: e.tensor_scalar(c1[:], zbuf[:, 0:TB], convw(0, i), None, ALU.mult), reads=[zbuf, prmT], writes=[c1])
                    P.op("dve", lambda e: e.scalar_tensor_tensor(out=c2[:], in0=zbuf[:, 1:TB + 1], scalar=convw(1, i), in1=c1[:], op0=ALU.mult, op1=ALU.add),
                         reads=[zbuf, c1, prmT], writes=[c2])
                    P.op("dve", lambda e: e.scalar_tensor_tensor(out=c1[:], in0=zbuf[:, 2:TB + 2], scalar=convw(2, i), in1=c2[:], op0=ALU.mult, op1=ALU.add),
                         reads=[zbuf, c2, prmT], writes=[c1])
                    P.op("dve", lambda e: e.tensor_tensor(out=ya[:], in0=c1[:], in1=abs_[:], op=ALU.mult), reads=[c1, abs_], writes=[ya])
                    y = ytile[yi[0] % 2]
                    yi[0] += 1
                    P.op("dve", lambda e: e.tensor_tensor(out=y[:], in0=ya[:], in1=sg[:], op=ALU.mult), reads=[ya, sg], writes=[y])
                    store_y(y, i)

                def hgrn_head(h):
                    fT, lf, kk, bb, eb, enb, dd, qin, kin, kdec, vT, osq, osb, sg = tmp[:14]
                    fT, vT, sg = hf2[h % 2], hv2[h % 2], hsg
                    wf = loadw(64 + h)
                    proj(wf, pp[0])
                    wv = loadw(80 + h)
                    proj(wv, pp[1])
                    if full:
                        wq = loadw(48 + h)
                        proj(wq, pp[2])
                        wg = loadw(112 + h)
                        proj(wg, pp[3])
                    P.op("act", lambda e: e.activation(out=fT[:], in_=pp[0][:], func=AF.Sigmoid), reads=[pp[0]], writes=[fT])
                    P.op("act", lambda e: e.copy(vT[:], pp[1][:]), reads=[pp[1]], writes=[vT])
                    if full:
                        P.op("act", lambda e: e.copy(qraw[:], pp[2][:]), reads=[pp[2]], writes=[qraw])
                        P.op("act", lambda e: e.activation(out=sg[:], in_=pp[3][:], func=AF.Silu), reads=[pp[3]], writes=[sg])
                    yield
                    P.op("dve", lambda e: e.tensor_scalar(fT[:], fT[:], omlb[:, h:h + 1], lbc[:, h:h + 1], ALU.mult, ALU.add), reads=[fT, omlb, lbc], writes=[fT])
                    P.op("act", lambda e: e.activation(out=lf[:], in_=fT[:], func=AF.Ln), reads=[fT], writes=[lf])
                    P.op("pool", lambda e: e.tensor_scalar(kk[:], fT[:], -1.0, 1.0, ALU.mult, ALU.add), reads=[fT], writes=[kk])
                    P.op("dve", lambda e: e.tensor_tensor_scan(out=bb[:], data0=cmask[:], data1=lf[:], initial=0.0, op0=ALU.mult, op1=ALU.add),
                         reads=[cmask, lf], writes=[bb])
                    bv = bb[:].rearrange("p (n c) -> p n c", c=32)
                    P.op("pool", lambda e: e.tensor_tensor(out=dd[:].rearrange("p (n c) -> p n c", c=32), in0=bv[:, :, 31:32].broadcast_to([128, 16, 32]),
                                                           in1=bv, op=ALU.subtract), reads=[bb], writes=[dd])
                    P.op("act", lambda e: e.activation(out=eb[:], in_=bb[:], func=AF.Exp), reads=[bb], writes=[eb])
                    P.op("act", lambda e: e.activation(out=dd[:], in_=dd[:], func=AF.Exp), reads=[dd], writes=[dd])
                    P.op("pool", lambda e: e.tensor_tensor(out=kdec[:], in0=kk[:], in1=dd[:], op=ALU.mult), reads=[kk, dd], writes=[kdec])
                    P.op("dve", lambda e: e.reduce_sum(out=osq[:, 0:1], in_=lf[:], axis=AX.X), reads=[lf], writes=[osq])
                    P.op("dve", lambda e: e.tensor_tensor(out=lsum[:, h:h + 1], in0=lsum[:, h:h + 1], in1=osq[:, 0:1], op=ALU.add), reads=[osq, lsum], writes=[lsum])
                    if full:
                        P.op("act", lambda e: e.activation(out=enb[:], in_=bb[:], func=AF.Exp, scale=-1.0), reads=[bb], writes=[enb])
                        P.op("dve", lambda e: e.tensor_tensor(out=qin[:], in0=eb[:], in1=qraw[:], op=ALU.mult), reads=[eb, qraw], writes=[qin])
                        P.op("pool", lambda e: e.tensor_tensor(out=kin[:], in0=kk[:], in1=enb[:], op=ALU.mult), reads=[kk, enb], writes=[kin])
                    for half in range(2):
                        for t2 in range(2):
                            tt = half * 2 + t2
                            P.op("pe", lambda e, tt=tt, t2=t2: e.transpose(pT[:, t2, 0:128], kdec[:, tt * 128:(tt + 1) * 128], ident[:]), reads=[kdec, ident], writes=[pT])
                            P.op("pe", lambda e, tt=tt, t2=t2: e.transpose(pT[:, t2, 128:256], vT[:, tt * 128:(tt + 1) * 128], ident[:]), reads=[vT, ident], writes=[pT])
                        P.op("act", lambda e, half=half: e.copy(kv[:, 2 * half:2 * half + 2, :], pT[:]), reads=[pT], writes=[kv])
                    if full:
                        for tt in range(4):
                            P.op("pe", lambda e, tt=tt: e.matmul(pS[:, tt, :], kin[:, tt * 128:(tt + 1) * 128], qin[:, tt * 128:(tt + 1) * 128], start=True, stop=True),
                                 reads=[kin, qin], writes=[pS])
                        P.op("dve", lambda e: e.tensor_tensor(out=AT[:], in0=pS[:], in1=maskT[:].unsqueeze(1).broadcast_to([128, 4, 128]), op=ALU.mult),
                             reads=[pS, maskT], writes=[AT])
                    for tt in range(4):
                        if full:
                            P.op("pe", lambda e, tt=tt: e.matmul(pO[:, tt * 128:(tt + 1) * 128], kv[:, tt, 128:256], AT[:, tt, :], start=True, stop=False),
                                 reads=[kv, AT], writes=[pO])
                        for n in range(4):
                            c0 = tt * 128 + n * 32
                            Sold = S[h][Spp[h]]
                            Snew = S[h][1 - Spp[h]]
                            if full:
                                P.op("pe", lambda e, c0=c0, Sold=Sold: e.matmul(pO[:, c0:c0 + 32], Sold[:], qin[:, c0:c0 + 32], start=False, stop=True),
                                     reads=[Sold, qin], writes=[pO])
                            P.op("pe", lambda e, tt=tt, n=n: e.matmul(pU[:], kv[32 * n:32 * n + 32, tt, 0:128], kv[32 * n:32 * n + 32, tt, 128:256], start=True, stop=True,
                                                                     tile_position=(32 * n, 0)),
                                 reads=[kv], writes=[pU])
                            P.op("dve", lambda e, c0=c0, Sold=Sold, Snew=Snew: e.scalar_tensor_tensor(out=Snew[:], in0=Sold[:], scalar=eb[:, c0 + 31:c0 + 32], in1=pU[:],
                                                                                                       op0=ALU.mult, op1=ALU.add),
                                 reads=[Sold, eb, pU], writes=[Snew])
                            Spp[h] = 1 - Spp[h]
                    if full:
                        P.op("act", lambda e: e.activation(out=osq[:], in_=pO[:], func=AF.Square), reads=[pO], writes=[osq])
                        P.op("dve", lambda e: e.tensor_copy(osb[:], pO[:]), reads=[pO], writes=[osb])
                        pR = pS
                        P.op("pe", lambda e: e.matmul(pR[:].rearrange("p a b -> p (a b)"), ones[:], osq[:], start=True, stop=True), reads=[ones, osq], writes=[pR])
                        rs = kk
                        P.op("act", lambda e: e.activation(out=rs[:], in_=pR[:].rearrange("p a b -> p (a b)"), func=AF.Ln, scale=1.0 / 128.0, bias=epsr[:, 0:1]), reads=[pR, epsr], writes=[rs])
                        P.op("act", lambda e: e.activation(out=rs[:], in_=rs[:], func=AF.Exp, scale=-0.5), reads=[rs], writes=[rs])
                        P.op("dve", lambda e: e.scalar_tensor_tensor(out=osb[:], in0=osb[:], scalar=hgn, in1=rs[:], op0=ALU.mult, op1=ALU.mult), reads=[osb, rs, prmT], writes=[osb])
                        y = ytile[yi[0] % 2]
                        yi[0] += 1
                        P.op("dve", lambda e: e.tensor_tensor(out=y[:], in0=osb[:], in1=sg[:], op=ALU.mult), reads=[osb, sg], writes=[y])
                        store_y(y, 16 + h)

                gens = []
                for i in range(16):
                    gens.append(hgrn_head(i))
                    if full:
                        gens.append(conv_tile(i))
                prev = None
                for gen in gens:
                    next(gen)
                    if prev is not None:
                        for _ in prev:
                            pass
                    prev = gen
                for _ in prev:
                    pass
                P.barrier()

            if not full:
                return
            phase_out(t0, yT_d, wo0, lambda j: xT_in[:, j, t0:t0 + TB], lng0, lnb0, lambda j: h1T_out[:, j, t0:t0 + TB], F32R)

        epsr = P.tile(g, "epsr", [128, 1])
        epsl = P.tile(g, "epsl", [128, 1])
        P.op("pool", lambda e: e.memset(epsr[:], RMS_EPS), writes=[epsr])
        P.op("pool", lambda e: e.memset(epsl[:], LN_EPS), writes=[epsl])

        lng1 = lambda j: prmT[:, 1, 32 + j:33 + j]
        lnb1 = lambda j: prmT[:, 1, 64 + j:65 + j]
        dsk = lambda j: prmT[:, 1, 96 + j:97 + j]
        bgl = lambda j: prmT[:, 2, j:j + 1]

        if mode == "A":
            for blk in range(NB):
                l0_block(blk, False)
            for h in range(16):
                P.dma("sp", sloc_out[:, h, :], S[h][Spp[h]][:], reads=[S[h][Spp[h]]], writes=[sloc_out])
            P.dma("sp", dtot_out[:], lsum[:], reads=[lsum], writes=[dtot_out])
        if mode == "B":
            for blk in range(NB):
                l0_block(blk, True)

        if mode in ("B", "C"):
            full1 = mode == "C"
            src_h1 = h1T_in if full1 else h1T_out
            PI = float(np.pi)
            NL = LOGNT + 1
            lam_s = P.tile(g, "lam_s", [128, 3, 128])
            pwr = P.tile(g, "pwr", [128, NL, 128])
            pwi = P.tile(g, "pwi", [128, NL, 128])
            npwi = P.tile(g, "npwi", [128, NL, 128])
            kr = P.tile(g, "kr", [128, 128])
            ki = P.tile(g, "ki", [128, 128])
            nki = P.tile(g, "nki", [128, 128])
            nkr = P.tile(g, "nkr", [128, 128])
            car = P.tile(g, "car", [128, 128])
            cai = P.tile(g, "cai", [128, 128])
            P.dma("sp", lam_s[:], lamT[:].rearrange("a p c -> p a c"), reads=[lamT], writes=[lam_s])
            with ExitStack() as es:
                tt_ = [P.tile(es, f"s5t{i}", [128, 128]) for i in range(10)]
                dt_, ar_, Lr, Li, mag, th, kk_, sn, cs, t9 = tt_
                P.op("act", lambda e: e.activation(out=dt_[:], in_=lam_s[:, 2, :], func=AF.Exp), reads=[lam_s], writes=[dt_])
                P.op("dve", lambda e: e.tensor_scalar(ar_[:], lam_s[:, 0, :], -1e-4, None, ALU.min), reads=[lam_s], writes=[ar_])
                P.op("dve", lambda e: e.tensor_tensor(out=Lr[:], in0=ar_[:], in1=dt_[:], op=ALU.mult), reads=[ar_, dt_], writes=[Lr])
                P.op("dve", lambda e: e.tensor_tensor(out=Li[:], in0=lam_s[:, 1, :], in1=dt_[:], op=ALU.mult), reads=[lam_s, dt_], writes=[Li])
                P.op("act", lambda e: e.activation(out=mag[:], in_=Lr[:], func=AF.Exp), reads=[Lr], writes=[mag])

                def sincos(dst, shift):
                    P.op("dve", lambda e: e.tensor_scalar(th[:], Li[:], float(shift), None, ALU.add), reads=[Li], writes=[th])
                    P.op("pool", lambda e: e.memset(kk_[:], 0.0), writes=[kk_])
                    for m in range(6):
                        P.op("dve", lambda e, m=m: e.tensor_scalar(t9[:], th[:], float((2 * m + 1) * PI), None, ALU.is_ge), reads=[th], writes=[t9])
                        P.op("dve", lambda e: e.tensor_tensor(out=kk_[:], in0=kk_[:], in1=t9[:], op=ALU.add), reads=[kk_, t9], writes=[kk_])
                    P.op("dve", lambda e: e.scalar_tensor_tensor(out=th[:], in0=kk_[:], scalar=float(-2 * PI), in1=th[:], op0=ALU.mult, op1=ALU.add), reads=[kk_, th], writes=[th])
                    P.op("act", lambda e: e.activation(out=dst[:], in_=th[:], func=AF.Sin), reads=[th], writes=[dst])
                sincos(sn, 0.0)
                sincos(cs, PI / 2)
                P.op("dve", lambda e: e.tensor_tensor(out=pwr[:, 0, :], in0=mag[:], in1=cs[:], op=ALU.mult), reads=[mag, cs], writes=[pwr])
                P.op("dve", lambda e: e.tensor_tensor(out=pwi[:, 0, :], in0=mag[:], in1=sn[:], op=ALU.mult), reads=[mag, sn], writes=[pwi])
                for k in range(NL - 1):
                    P.op("dve", lambda e, k=k: e.tensor_tensor(out=dt_[:], in0=pwr[:, k, :], in1=pwr[:, k, :], op=ALU.mult), reads=[pwr], writes=[dt_])
                    P.op("dve", lambda e, k=k: e.tensor_tensor(out=mag[:], in0=pwi[:, k, :], in1=pwi[:, k, :], op=ALU.mult), reads=[pwi], writes=[mag])
                    P.op("dve", lambda e, k=k: e.tensor_tensor(out=pwr[:, k + 1, :], in0=dt_[:], in1=mag[:], op=ALU.subtract), reads=[dt_, mag, pwr], writes=[pwr])
                    P.op("dve", lambda e, k=k: e.scalar_tensor_tensor(out=pwi[:, k + 1, :], in0=pwr[:, k, :], scalar=2.0, in1=pwi[:, k, :], op0=ALU.mult, op1=ALU.mult),
                         reads=[pwr, pwi], writes=[pwi])
                P.op("dve", lambda e: e.tensor_scalar(npwi[:], pwi[:], -1.0, None, ALU.mult), reads=[pwi], writes=[npwi])
                lm1 = dt_
                P.op("dve", lambda e: e.tensor_scalar(lm1[:], pwr[:, 0, :], -1.0, None, ALU.add), reads=[pwr], writes=[lm1])
                den = mag
                P.op("dve", lambda e: e.tensor_tensor(out=den[:], in0=ar_[:], in1=ar_[:], op=ALU.mult), reads=[ar_], writes=[den])
                P.op("dve", lambda e: e.tensor_tensor(out=t9[:], in0=lam_s[:, 1, :], in1=lam_s[:, 1, :], op=ALU.mult), reads=[lam_s], writes=[t9])
                P.op("dve", lambda e: e.tensor_tensor(out=den[:], in0=den[:], in1=t9[:], op=ALU.add), reads=[den, t9], writes=[den])
                P.op("dve", lambda e: e.reciprocal(den[:], den[:]), reads=[den], writes=[den])
                P.op("dve", lambda e: e.tensor_tensor(out=th[:], in0=lm1[:], in1=ar_[:], op=ALU.mult), reads=[lm1, ar_], writes=[th])
                P.op("dve", lambda e: e.tensor_tensor(out=t9[:], in0=pwi[:, 0, :], in1=lam_s[:, 1, :], op=ALU.mult), reads=[pwi, lam_s], writes=[t9])
                P.op("dve", lambda e: e.tensor_tensor(out=th[:], in0=th[:], in1=t9[:], op=ALU.add), reads=[th, t9], writes=[th])
                P.op("dve", lambda e: e.tensor_tensor(out=kr[:], in0=th[:], in1=den[:], op=ALU.mult), reads=[th, den], writes=[kr])
                P.op("dve", lambda e: e.tensor_tensor(out=th[:], in0=pwi[:, 0, :], in1=ar_[:], op=ALU.mult), reads=[pwi, ar_], writes=[th])
                P.op("dve", lambda e: e.tensor_tensor(out=t9[:], in0=lm1[:], in1=lam_s[:, 1, :], op=ALU.mult), reads=[lm1, lam_s], writes=[t9])
                P.op("dve", lambda e: e.tensor_tensor(out=th[:], in0=th[:], in1=t9[:], op=ALU.subtract), reads=[th, t9], writes=[th])
                P.op("dve", lambda e: e.tensor_tensor(out=ki[:], in0=th[:], in1=den[:], op=ALU.mult), reads=[th, den], writes=[ki])
                P.op("dve", lambda e: e.tensor_scalar(nki[:], ki[:], -1.0, None, ALU.mult), reads=[ki], writes=[nki])
                P.op("dve", lambda e: e.tensor_scalar(nkr[:], kr[:], -1.0, None, ALU.mult), reads=[kr], writes=[nkr])
                P.op("pool", lambda e: e.memset(car[:], 0.0), writes=[car])
                P.op("pool", lambda e: e.memset(cai[:], 0.0), writes=[cai])
                if full1:
                    xsl = P.tile(es, "xsl", [128, 2, 128])
                    for a in range(3):
                        P.dma("sp", xsl[:], xs_in[a].rearrange("r p c -> p r c"), reads=[xs_in], writes=[xsl])
                        P.op("dve", lambda e: e.tensor_tensor(out=sn[:], in0=car[:], in1=pwr[:, LOGNT, :], op=ALU.mult), reads=[car, pwr], writes=[sn])
                        P.op("dve", lambda e: e.tensor_tensor(out=cs[:], in0=cai[:], in1=pwi[:, LOGNT, :], op=ALU.mult), reads=[cai, pwi], writes=[cs])
                        P.op("dve", lambda e: e.tensor_tensor(out=sn[:], in0=sn[:], in1=cs[:], op=ALU.subtract), reads=[sn, cs], writes=[sn])
                        P.op("dve", lambda e: e.tensor_tensor(out=cs[:], in0=car[:], in1=pwi[:, LOGNT, :], op=ALU.mult), reads=[car, pwi], writes=[cs])
                        P.op("dve", lambda e: e.tensor_tensor(out=th[:], in0=cai[:], in1=pwr[:, LOGNT, :], op=ALU.mult), reads=[cai, pwr], writes=[th])
                        P.op("dve", lambda e: e.tensor_tensor(out=cs[:], in0=cs[:], in1=th[:], op=ALU.add), reads=[cs, th], writes=[cs])
                        P.op("dve", lambda e: e.tensor_tensor(out=car[:], in0=sn[:], in1=xsl[:, 0, :], op=ALU.add), reads=[sn, xsl], writes=[car])
                        P.op("dve", lambda e: e.tensor_tensor(out=cai[:], in0=cs[:], in1=xsl[:, 1, :], op=ALU.add), reads=[cs, xsl], writes=[cai])
                P.barrier()

            def l1_block(blk):
                t0 = blk * TB
                with ExitStack() as es:
                    xT = P.tile(es, "xT1", [128, KT, TB], F32R)
                    wb = [P.tile(es, f"wc{i}", [128, KT, 128], F32R) for i in range(2)]
                    wi = [0]
                    uT = P.tile(es, "uT", [128, TB])
                    srs = [[P.tile(es, f"sr{q}{i}", [128, TB]) for i in range(2)] for q in range(2)]
                    sis = [[P.tile(es, f"si{q}{i}", [128, TB]) for i in range(2)] for q in range(2)]
                    t1 = P.tile(es, "t1", [128, TB])
                    cc = P.tile(es, "cc", [128, 4])
                    gt = [P.tile(es, f"gt{i}", [128, TB], F32R) for i in range(2)]
                    sgt = [P.tile(es, f"sgt{i}", [128, TB]) for i in range(2)]
                    Bt2 = [P.tile(es, f"Bt{i}", [128, 2, 128]) for i in range(2)]
                    Ct2 = [P.tile(es, f"Ct{i}", [128, 2, 4, 32]) for i in range(2)]
                    Cp2 = [P.tile(es, f"Cp{i}", [128, 2, 32]) for i in range(2)]
                    ctmp = P.tile(es, "ctmp", [128, 32])
                    pu = P.ptile(es, "pu", [128, TB])
                    pg = P.ptile(es, "pg", [128, TB])
                    pbrs = [P.ptile(es, f"pbr{q}", [128, TB]) for q in range(2)]
                    pbis = [P.ptile(es, f"pbi{q}", [128, TB]) for q in range(2)]
                    pY = P.ptile(es, "pY", [128, TB])
                    for k4 in range(4):
                        P.dma("sp", xT[:, 8 * k4:8 * k4 + 8, :], src_h1[:, 8 * k4:8 * k4 + 8, t0:t0 + TB], reads=[], writes=[xT])

                    def loadw(chunk):
                        w = wb[wi[0] % 2]
                        wi[0] += 1
                        P.dma("sp", w[:], w1[chunk], reads=[], writes=[w])
                        return w

                    def proj(w, ps):
                        for k in range(KT):
                            P.op("pe", lambda e, k=k: e.matmul(ps[:], w[:, k, :], xT[:, k, :], start=(k == 0), stop=(k == KT - 1)), reads=[w, xT], writes=[ps])

                    for i in range(32):
                        Bsb = Bt2[i % 2]
                        P.dma("sp", Bsb[:], Bh[:, :, i, :].rearrange("a p c -> p a c"), reads=[], writes=[Bsb])
                        if full1:
                            Csb = Ct2[i % 2]
                            P.dma("sp", Csb[:], Ch[:, :, 4 * i:4 * i + 4, :].rearrange("a p i c -> p a i c"), reads=[], writes=[Csb])
                        wu = loadw(i)
                        proj(wu, pu)
                        if full1:
                            wgt = loadw(32 + i)
                            proj(wgt, pg)
                        P.op("act", lambda e: e.copy(uT[:], pu[:]), reads=[pu], writes=[uT])
                        for j in range(4):
                            pair = 4 * i + j
                            pc = slice(pair, pair + 1)
                            rows = slice(32 * j, 32 * j + 32)
                            sr, si, pbr, pbi = srs[pair % 2], sis[pair % 2], pbrs[pair % 2], pbis[pair % 2]
                            P.op("pe", lambda e: e.matmul(pbr[:], Bsb[rows, 0, :], uT[rows, :], start=True, stop=True, tile_position=(32 * j, 0)), reads=[Bsb, uT], writes=[pbr])
                            P.op("pe", lambda e: e.matmul(pbi[:], Bsb[rows, 1, :], uT[rows, :], start=True, stop=True, tile_position=(32 * j, 0)), reads=[Bsb, uT], writes=[pbi])
                            a, b_ = 0, 1
                            P.op("act", lambda e: e.copy(sr[a][:], pbr[:]), reads=[pbr], writes=[sr[a]])
                            P.op("act", lambda e: e.copy(si[a][:], pbi[:]), reads=[pbi], writes=[si[a]])
                            if full1:
                                Cp = Cp2[pair % 2]
                                P.op("dve", lambda e: e.tensor_scalar(ctmp[:], Csb[:, 0, j, :], kr[:, pc], None, ALU.mult), reads=[Csb, kr], writes=[ctmp])
                                P.op("dve", lambda e: e.scalar_tensor_tensor(out=Cp[:, 0, :], in0=Csb[:, 1, j, :], scalar=nki[:, pc], in1=ctmp[:], op0=ALU.mult, op1=ALU.add), reads=[Csb, nki, ctmp], writes=[Cp])
                                P.op("dve", lambda e: e.tensor_scalar(ctmp[:], Csb[:, 0, j, :], nki[:, pc], None, ALU.mult), reads=[Csb, nki], writes=[ctmp])
                                P.op("dve", lambda e: e.scalar_tensor_tensor(out=Cp[:, 1, :], in0=Csb[:, 1, j, :], scalar=nkr[:, pc], in1=ctmp[:], op0=ALU.mult, op1=ALU.add), reads=[Csb, nkr, ctmp], writes=[Cp])
                            P.op("dve", lambda e: e.tensor_tensor(out=cc[:, 0:1], in0=car[:, pc], in1=pwr[:, 0, pc], op=ALU.mult), reads=[car, pwr], writes=[cc])
                            P.op("dve", lambda e: e.scalar_tensor_tensor(out=cc[:, 0:1], in0=cai[:, pc], scalar=npwi[:, 0, pc], in1=cc[:, 0:1], op0=ALU.mult, op1=ALU.add), reads=[cai, npwi, cc], writes=[cc])
                            P.op("dve", lambda e: e.tensor_tensor(out=cc[:, 1:2], in0=cai[:, pc], in1=pwr[:, 0, pc], op=ALU.mult), reads=[cai, pwr, cc], writes=[cc])
                            P.op("dve", lambda e: e.scalar_tensor_tensor(out=cc[:, 1:2], in0=car[:, pc], scalar=pwi[:, 0, pc], in1=cc[:, 1:2], op0=ALU.mult, op1=ALU.add), reads=[car, pwi, cc], writes=[cc])
                            P.op("dve", lambda e: e.tensor_tensor(out=sr[a][:, 0:1], in0=sr[a][:, 0:1], in1=cc[:, 0:1], op=ALU.add), reads=[sr[a], cc], writes=[sr[a]])
                            P.op("dve", lambda e: e.tensor_tensor(out=si[a][:, 0:1], in0=si[a][:, 0:1], in1=cc[:, 1:2], op=ALU.add), reads=[si[a], cc], writes=[si[a]])
                            if full1:
                                for k in range(9):
                                    sh = 1 << k
                                    A_r, A_i, B_r, B_i = sr[a], si[a], sr[b_], si[b_]
                                    P.op("dve", lambda e: e.tensor_copy(B_r[:, 0:sh], A_r[:, 0:sh]), reads=[A_r, B_r], writes=[B_r], nosame=True)
                                    P.op("dve", lambda e: e.tensor_copy(B_i[:, 0:sh], A_i[:, 0:sh]), reads=[A_i, B_i], writes=[B_i], nosame=True)
                                    P.op("dve", lambda e: e.scalar_tensor_tensor(out=t1[:, sh:TB], in0=A_r[:, 0:TB - sh], scalar=pwr[:, k, pc], in1=A_r[:, sh:TB], op0=ALU.mult, op1=ALU.add),
                                         reads=[A_r, pwr], writes=[t1], nosame=True)
                                    P.op("dve", lambda e: e.scalar_tensor_tensor(out=B_r[:, sh:TB], in0=A_i[:, 0:TB - sh], scalar=npwi[:, k, pc], in1=t1[:, sh:TB], op0=ALU.mult, op1=ALU.add),
                                         reads=[A_i, npwi, t1, B_r], writes=[B_r], nosame=True)
                                    P.op("dve", lambda e: e.scalar_tensor_tensor(out=t1[:, sh:TB], in0=A_i[:, 0:TB - sh], scalar=pwr[:, k, pc], in1=A_i[:, sh:TB], op0=ALU.mult, op1=ALU.add),
                                         reads=[A_i, pwr], writes=[t1], nosame=True)
                                    P.op("dve", lambda e: e.scalar_tensor_tensor(out=B_i[:, sh:TB], in0=A_r[:, 0:TB - sh], scalar=pwi[:, k, pc], in1=t1[:, sh:TB], op0=ALU.mult, op1=ALU.add),
                                         reads=[A_r, pwi, t1, B_i], writes=[B_i], nosame=True)
                                    a, b_ = b_, a

                                lastc = slice(TB - 1, TB)
                            else:
                                n = TB
                                for k in range(9):
                                    n //= 2
                                    A_r, A_i, B_r, B_i = sr[a], si[a], sr[b_], si[b_]
                                    ev = slice(0, 2 * n, 2)
                                    od = slice(1, 2 * n, 2)
                                    P.op("dve", lambda e: e.scalar_tensor_tensor(out=t1[:, 0:n], in0=A_r[:, ev], scalar=pwr[:, k, pc], in1=A_r[:, od], op0=ALU.mult, op1=ALU.add),
                                         reads=[A_r, pwr], writes=[t1], nosame=True)
                                    P.op("dve", lambda e: e.scalar_tensor_tensor(out=B_r[:, 0:n], in0=A_i[:, ev], scalar=npwi[:, k, pc], in1=t1[:, 0:n], op0=ALU.mult, op1=ALU.add),
                                         reads=[A_i, npwi, t1, B_r], writes=[B_r], nosame=True)
                                    P.op("dve", lambda e: e.scalar_tensor_tensor(out=t1[:, 0:n], in0=A_i[:, ev], scalar=pwr[:, k, pc], in1=A_i[:, od], op0=ALU.mult, op1=ALU.add),
                                         reads=[A_i, pwr, t1], writes=[t1], nosame=True)
                                    P.op("dve", lambda e: e.scalar_tensor_tensor(out=B_i[:, 0:n], in0=A_r[:, ev], scalar=pwi[:, k, pc], in1=t1[:, 0:n], op0=ALU.mult, op1=ALU.add),
                                         reads=[A_r, pwi, t1, B_i], writes=[B_i], nosame=True)
                                    a, b_ = b_, a
                                lastc = slice(0, 1)
                            X_r, X_i = sr[a], si[a]
                            P.op("dve", lambda e: e.tensor_copy(car[:, pc], X_r[:, lastc]), reads=[X_r, car], writes=[car], nosame=True)
                            P.op("dve", lambda e: e.tensor_copy(cai[:, pc], X_i[:, lastc]), reads=[X_i, cai], writes=[cai], nosame=True)
                            if full1:
                                P.op("pe", lambda e: e.matmul(pY[rows, :], Cp[:, 0, :], X_r[:], start=True, stop=False, tile_position=(0, 32 * j)), reads=[Cp, X_r], writes=[pY])
                                P.op("pe", lambda e: e.matmul(pY[rows, :], Cp[:, 1, :], X_i[:], start=False, stop=True, tile_position=(0, 32 * j)), reads=[Cp, X_i], writes=[pY])
                        if full1:
                            g_ = gt[i % 2]
                            s_ = sgt[i % 2]
                            P.op("dve", lambda e: e.scalar_tensor_tensor(out=t1[:], in0=uT[:], scalar=dsk(i), in1=pY[:], op0=ALU.mult, op1=ALU.add), reads=[uT, prmT, pY], writes=[t1])
                            P.op("act", lambda e: e.activation(out=g_[:], in_=t1[:], func=AF.Gelu_apprx_tanh), reads=[t1], writes=[g_])
                            P.op("act", lambda e: e.activation(out=s_[:], in_=pg[:], func=AF.Silu), reads=[pg], writes=[s_])
                            P.dma("pool", yT_d[i], g_[:], reads=[g_], writes=[Buf("yT_d")])
                            P.dma("pool", sg_d[i], s_[:], reads=[s_], writes=[Buf("sg_d")])
                    P.barrier()
                if not full1:
                    return
                with ExitStack() as es:
                    gT = P.tile(es, "gT", [128, KT, TB], F32R)
                    wb = [P.tile(es, f"wd{i}", [128, KT, 128], F32R) for i in range(3)]
                    sgl = [P.tile(es, f"sgl{i}", [128, TB]) for i in range(2)]
                    zz = [P.tile(es, f"zz{i}", [128, TB]) for i in range(2)]
                    y2 = [P.tile(es, f"y2{i}", [128, TB], F32R) for i in range(2)]
                    pz = [P.ptile(es, f"pz{i}", [128, TB]) for i in range(2)]
                    for k4 in range(4):
                        P.dma("sp", gT[:, 8 * k4:8 * k4 + 8, :], yT_d[8 * k4:8 * k4 + 8].rearrange("k p t -> p k t"), reads=[], writes=[gT])
                    for j in range(KT):
                        w = wb[j % 3]
                        P.dma("sp", w[:], wg1[j], reads=[], writes=[w])
                        sl_ = sgl[j % 2]
                        P.dma("sp", sl_[:], sg_d[j], reads=[], writes=[sl_])
                        ps = pz[j % 2]
                        for k in range(KT):
                            P.op("pe", lambda e, k=k: e.matmul(ps[:], w[:, k, :], gT[:, k, :], start=(k == 0), stop=(k == KT - 1)), reads=[w, gT], writes=[ps])
                        z_ = zz[j % 2]
                        P.op("act", lambda e: e.activation(out=z_[:], in_=ps[:], func=AF.Sigmoid, bias=bgl(j)), reads=[ps, prmT], writes=[z_])
                        P.op("dve", lambda e: e.tensor_tensor(out=z_[:], in0=z_[:], in1=gT[:, j, :].bitcast(F32), op=ALU.mult), reads=[z_, gT], writes=[z_])
                        o_ = y2[j % 2]
                        P.op("dve", lambda e: e.tensor_tensor(out=o_[:], in0=z_[:], in1=sl_[:], op=ALU.mult), reads=[z_, sl_], writes=[o_])
                        P.dma("pool", y2_d[j], o_[:], reads=[o_], writes=[Buf("y2_d")])
                    P.barrier()
                phase_out(t0, y2_d, wo1, lambda j: h1T_in[:, j, t0:t0 + TB], lng1, lnb1, lambda j: outT[:, j, t0:t0 + TB], F32)

            for blk in range(NB):
                l1_block(blk)
            if not full1:
                P.dma("sp", xloc_out[0], car[:], reads=[car], writes=[xloc_out])
                P.dma("sp", xloc_out[1], cai[:], reads=[cai], writes=[xloc_out])
        P.barrier()
    print("instructions:", P.ninst, flush=True)
    return nc


def _chunks(w, idx_tiles):
    E = w.shape[1]
    wt = w.reshape(KT, 128, E // 128, 128)
    wt = wt[:, :, idx_tiles, :]
    return np.ascontiguousarray(wt.transpose(2, 1, 0, 3))


def _scan_layout(a):
    return np.ascontiguousarray(a.reshape(128, 2, 64).transpose(1, 2, 0).reshape(128, 128))


def _prep_s5(inp):
    lam = np.stack([_scan_layout(inp["od_lam_re"][0]), _scan_layout(inp["od_lam_im"][0]),
                    _scan_layout(np.broadcast_to(inp["od_log_step"][0][:, None], (256, 64)))]).astype(np.float32)
    Bh = np.zeros((2, 128, 32, 128), np.float32)
    Ch = np.zeros((2, 128, 128, 32), np.float32)
    for r, (bk, ck) in enumerate([("od_b_re", "od_c_re"), ("od_b_im", "od_c_im")]):
        b = inp[bk][0]
        c = inp[ck][0]
        for gq in range(256):
            i, gl = gq // 8, gq % 8
            g2 = gl % 2
            Bh[r, gl * 16:(gl + 1) * 16, i, g2 * 64:(g2 + 1) * 64] = b[gq].T
            Ch[r, g2 * 64:(g2 + 1) * 64, gq // 2, g2 * 16:(g2 + 1) * 16] = c[gq].T
    return lam, Bh, Ch


def run_pipeline(inp, x, NT):
    ids = list(range(NCORES))
    prm = pack_params(inp)
    xTs, xhTs = [], []
    for c in range(NCORES):
        b, sg = c // 4, c % 4
        xc = x[b, sg * NT:(sg + 1) * NT]
        xTs.append(np.ascontiguousarray(xc.reshape(NT, 32, 128).transpose(2, 1, 0)))
        xh = np.zeros((2, D), np.float32) if sg == 0 else x[b, sg * NT - 2:sg * NT]
        xhTs.append(np.ascontiguousarray(xh.reshape(2, 32, 128).transpose(2, 1, 0)))
    w0 = _chunks(inp["ev_w_in"][0], list(range(128)))
    ncA = build(NT, "A")
    wA = np.ascontiguousarray(w0[64:96])
    rA = run_bass_kernel_spmd(ncA, [{"prm": prm, "xT": xTs[c], "w0": wA} for c in ids], core_ids=ids).results
    wo0 = _chunks(inp["ev_w_out"][0], list(range(32)))
    w1 = _chunks(inp["od_w_in"][0], list(range(64)))
    lam, Bh, Ch = _prep_s5(inp)
    mapsB = []
    for c in ids:
        sg = c % 4
        slS = np.zeros((3, 128, 16, 128), np.float32)
        slD = np.zeros((3, 128, 16), np.float32)
        for j in range(sg):
            slS[3 - sg + j] = rA[c - sg + j]["sloc"]
            slD[3 - sg + j] = rA[c - sg + j]["dtot"]
        mapsB.append({"prm": prm, "xT": xTs[c], "xhT": xhTs[c], "w0": w0, "wo0": wo0, "sl_S": slS, "sl_D": slD,
                      "w1": np.ascontiguousarray(w1[:32]), "lamT": lam, "Bh": Bh, "Ch": Ch})
    ncB = build(NT, "B")
    rB = run_bass_kernel_spmd(ncB, mapsB, core_ids=ids).results
    del w0, wo0, mapsB
    wg1 = _chunks(inp["od_w_glu"][0], list(range(32)))
    wo1 = _chunks(inp["od_w_out"][0], list(range(32)))
    mapsC = []
    for c in ids:
        sg = c % 4
        xs = np.zeros((3, 2, 128, 128), np.float32)
        for j in range(sg):
            xs[3 - sg + j] = rB[c - sg + j]["xloc"]
        mapsC.append({"prm": prm, "h1T": rB[c]["h1T"], "w1": w1, "wg1": wg1, "wo1": wo1, "xs": xs, "lamT": lam, "Bh": Bh, "Ch": Ch})
    ncC = build(NT, "C")
    rC = run_bass_kernel_spmd(ncC, mapsC, core_ids=ids).results
    out = np.zeros((2, 4 * NT, D), np.float32)
    for c in ids:
        b, sg = c // 4, c % 4
        out[b, sg * NT:(sg + 1) * NT] = rC[c]["outT"].transpose(2, 1, 0).reshape(NT, D)
    return out, rB


def kernel(**inp):
    inp = {k: np.asarray(v) for k, v in inp.items()}
    out, _ = run_pipeline(inp, inp["x"], 2048)
    return out


def pack_params(inp):
    prm = np.zeros((3, 128, 128), np.float32)
    prm[0, 0:48] = inp["hg_lb_logits"].reshape(48, 128)
    prm[0, 48:96] = inp["ev_conv_w"][0].reshape(48, 128)
    prm[0, 96:128] = inp["ev_ln_g"][0].reshape(32, 128)
    prm[1, 0:32] = inp["ev_ln_b"][0].reshape(32, 128)
    prm[1, 32:64] = inp["od_ln_g"][0].reshape(32, 128)
    prm[1, 64:96] = inp["od_ln_b"][0].reshape(32, 128)
    prm[1, 96:128] = inp["od_d"][0].reshape(32, 128)
    prm[2, 0:32] = inp["od_b_glu"][0].reshape(32, 128)
    prm[2, 32] = inp["ev_hg_norm"][0]
    return prm
```

```python
from contextlib import ExitStack
import numpy as np
import concourse.bass as bass
import concourse.mybir as mybir
from concourse.bass_utils import run_bass_kernel_spmd

F32 = mybir.dt.float32
F32R = mybir.dt.float32r
AF = mybir.ActivationFunctionType
ALU = mybir.AluOpType
AX = mybir.AxisListType

D = 4096
KT = 32
TB = 512
NCORES = 8
ALPHA = 4.0 ** 0.25
LN_EPS = 1e-5
RMS_EPS = 1e-6


class Buf:
    def __init__(self, name):
        self.name = name
        self.w = None
        self.r = {}
        self.dsem = None
        self.dcnt = 0


class T:
    def __init__(self, t, name):
        self.t = t
        self.b = Buf(name)

    def __getitem__(self, k):
        return self.t[k]


class Prog:
    def __init__(self, nc):
        self.nc = nc
        self.eng = {"pe": nc.tensor, "act": nc.scalar, "dve": nc.vector, "pool": nc.gpsimd, "sp": nc.sync}
        self.sem = {e: nc.alloc_semaphore("s_" + e) for e in ["pe", "act", "dve", "pool"]}
        self.cnt = {e: 0 for e in self.sem}
        self.waited = {}
        self.dsems = {}
        self.all_dsems = []
        self.ninst = 0

    def tile(self, es, name, shape, dtype=F32):
        self.uid = getattr(self, "uid", 0) + 1
        t = es.enter_context(self.nc.sbuf_tensor(f"{name}_{self.uid}", shape, dtype))
        return T(t, name)

    def ptile(self, es, name, shape, dtype=F32):
        self.uid = getattr(self, "uid", 0) + 1
        t = es.enter_context(self.nc.psum_tensor(f"{name}_{self.uid}", shape, dtype))
        return T(t, name)

    def dram(self, name, shape, dtype=F32, kind="Internal"):
        t = self.nc.dram_tensor(name, shape, dtype, kind=kind)
        return T(t.ap(), name)

    def _wait(self, e, tok):
        sem, val = tok
        key = (e, sem.num)
        if self.waited.get(key, 0) >= val:
            return
        self.waited[key] = val
        self.eng[e].wait_ge(sem, val)

    def _deps(self, e, reads, writes, nosame=False):
        toks = []
        for b in reads:
            if b.w:
                toks.append(b.w)
        for b in writes:
            if b.w:
                toks.append(b.w)
            toks.extend(b.r.values())
        for t in toks:
            if t[0] is self.sem.get(e) and (e == "pe" or nosame):
                continue
            self._wait(e, t)

    def _post(self, tok, reads, writes):
        for b in writes:
            b.w = tok
            b.r = {}
        for b in reads:
            if b not in writes:
                b.r[tok[0].num] = tok

    def op(self, e, fn, reads=(), writes=(), nosame=False):
        reads = [x.b if isinstance(x, T) else x for x in reads]
        writes = [x.b if isinstance(x, T) else x for x in writes]
        self._deps(e, reads, writes, nosame)
        inst = fn(self.eng[e])
        self.cnt[e] += 1
        self.ninst += 1
        inst.then_inc(self.sem[e], 1)
        tok = (self.sem[e], self.cnt[e])
        self._post(tok, reads, writes)

    def dma(self, e, out_ap, in_ap, reads=(), writes=()):
        reads = [x.b if isinstance(x, T) else x for x in reads]
        writes = [x.b if isinstance(x, T) else x for x in writes]
        self._deps(e, reads, writes)
        sb = writes[0]
        if sb.dsem is None:
            if sb.name not in self.dsems:
                self.dsems[sb.name] = [self.nc.alloc_semaphore("d_" + sb.name), 0]
                self.all_dsems.append(self.dsems[sb.name])
            sb.dsem = self.dsems[sb.name]
        sb.dsem[1] += 16
        self.eng[e].dma_start(out=out_ap, in_=in_ap).then_inc(sb.dsem[0], 16)
        self.ninst += 1
        tok = (sb.dsem[0], sb.dsem[1])
        self._post(tok, reads, writes)

    def barrier(self):
        for e in ["pe", "act", "dve", "pool", "sp"]:
            for e2 in self.sem:
                if e2 != e and self.cnt[e2] > 0:
                    self._wait(e, (self.sem[e2], self.cnt[e2]))
            for ds in self.all_dsems:
                if ds[1] > 0:
                    self._wait(e, (ds[0], ds[1]))


def build(NT, mode):
    NB = NT // TB
    nc = bass.Bass("TRN2", target_bir_lowering=False)
    nc.dge_precook = False
    P = Prog(nc)
    din = lambda name, shape, dt=F32: T(nc.dram_tensor(name, shape, dt, kind="ExternalInput").ap(), name)
    dout = lambda name, shape, dt=F32: T(nc.dram_tensor(name, shape, dt, kind="ExternalOutput").ap(), name)

    LOGNT = NT.bit_length() - 1
    prm = din("prm", [3, 128, 128], F32)
    if mode in ("A", "B"):
        xT_in = din("xT", [128, KT, NT], F32R)
    if mode == "A":
        w0 = din("w0", [32, 128, KT, 128], F32R)
        sloc_out = dout("sloc", [128, 16, 128], F32)
        dtot_out = dout("dtot", [128, 16], F32)
        WMAP = lambda c: c - 64
    if mode == "B":
        xhT_in = din("xhT", [128, KT, 2], F32)
        w0 = din("w0", [128, 128, KT, 128], F32R)
        wo0 = din("wo0", [32, 128, KT, 128], F32R)
        sl_S = din("sl_S", [3, 128, 16, 128], F32)
        sl_D = din("sl_D", [3, 128, 16], F32)
        h1T_out = dout("h1T", [128, KT, NT], F32R)
        xloc_out = dout("xloc", [2, 128, 128], F32)
        WMAP = lambda c: c
    if mode == "C":
        h1T_in = din("h1T", [128, KT, NT], F32R)
        wg1 = din("wg1", [32, 128, KT, 128], F32R)
        wo1 = din("wo1", [32, 128, KT, 128], F32R)
        xs_in = din("xs", [3, 2, 128, 128], F32)
        outT = dout("outT", [128, KT, NT], F32)
    if mode in ("B", "C"):
        w1 = din("w1", [64 if mode == "C" else 32, 128, KT, 128], F32R)
        lamT = din("lamT", [3, 128, 128], F32)
        Bh = din("Bh", [2, 128, 32, 128], F32)
        Ch = din("Ch", [2, 128, 128, 32], F32)

    yT_d = P.dram("yT_d", [32, 128, TB], F32R)
    hp_d = P.dram("hp_d", [32, 128, TB], F32)
    sg_d = P.dram("sg_d", [32, 128, TB], F32)
    y2_d = P.dram("y2_d", [32, 128, TB], F32R)

    with ExitStack() as g:
        ident = P.tile(g, "ident", [128, 128])
        ones = P.tile(g, "ones", [128, 128])
        cmask = P.tile(g, "cmask", [128, TB])
        maskT = P.tile(g, "maskT", [128, 128])
        prmT = P.tile(g, "prmT", [128, 3, 128])
        prm_s = P.tile(g, "prm_s", [128, 3, 128])
        lbc = P.tile(g, "lbc", [128, 16])
        omlb = P.tile(g, "omlb", [128, 16])
        SD = 128 if mode != "C" else 1
        S = [[P.tile(g, f"S{h}_{i}", [128, SD]) for i in range(2)] for h in range(16)]
        Spp = [0] * 16
        zhalo = P.tile(g, "zhalo", [128, 16, 2])
        lsum = P.tile(g, "lsum", [128, 16])

        P.op("pool", lambda e: e.memset(ident[:], 0.0), writes=[ident])
        P.op("pool", lambda e: e.affine_select(out=ident[:], in_=ident[:], pattern=[[-1, 128]], compare_op=ALU.not_equal,
                                               fill=1.0, base=0, channel_multiplier=1), reads=[ident], writes=[ident])
        P.op("pool", lambda e: e.memset(ones[:], 1.0), writes=[ones])
        P.op("pool", lambda e: e.memset(cmask[:], 1.0), writes=[cmask])
        P.op("pool", lambda e: e.memset(cmask[:].rearrange("p (n c) -> p n c", c=32)[:, :, 0:1], 0.0), reads=[cmask], writes=[cmask])
        P.op("pool", lambda e: e.memset(maskT[:], 1.0), writes=[maskT])
        P.op("pool", lambda e: e.affine_select(out=maskT[:], in_=maskT[:], pattern=[[1, 128]], compare_op=ALU.is_ge,
                                               fill=0.0, base=0, channel_multiplier=-1), reads=[maskT], writes=[maskT])
        for i in range(3):
            P.op("pool", lambda e, i=i: e.memset(maskT[32 * i:32 * i + 32, 32 * (i + 1):128], 0.0), reads=[maskT], writes=[maskT])
        P.op("pool", lambda e: e.memset(lsum[:], 0.0), writes=[lsum])

        P.dma("sp", prm_s[:], prm[:].rearrange("a r c -> r a c"), reads=[prm], writes=[prm_s])
        with ExitStack() as es:
            pt = P.ptile(es, "pt0", [128, 3, 128])
            for a in range(3):
                P.op("pe", lambda e, a=a: e.transpose(pt[:, a, :], prm_s[:, a, :], ident[:]), reads=[prm_s, ident], writes=[pt])
            P.op("dve", lambda e: e.tensor_copy(prmT[:], pt[:]), reads=[pt], writes=[prmT])
            ex = P.tile(es, "ex", [128, 48])
            sm = P.tile(es, "sm", [128, 16])
            P.op("act", lambda e: e.activation(out=ex[:], in_=prmT[:, 0, 0:48], func=AF.Exp), reads=[prmT], writes=[ex])
            P.op("dve", lambda e: e.tensor_tensor(out=sm[:], in0=ex[:, 0:16], in1=ex[:, 16:32], op=ALU.add), reads=[ex], writes=[sm])
            P.op("dve", lambda e: e.tensor_tensor(out=sm[:], in0=sm[:], in1=ex[:, 32:48], op=ALU.add), reads=[ex, sm], writes=[sm])
            P.op("dve", lambda e: e.reciprocal(sm[:], sm[:]), reads=[sm], writes=[sm])
            P.op("dve", lambda e: e.tensor_tensor(out=lbc[:], in0=ex[:, 0:16], in1=sm[:], op=ALU.mult), reads=[ex, sm], writes=[lbc])
            P.op("dve", lambda e: e.tensor_scalar(omlb[:], lbc[:], -1.0, 1.0, ALU.mult, ALU.add), reads=[lbc], writes=[omlb])
            P.barrier()
        convw = lambda k, t: prmT[:, 0, 48 + k * 16 + t:48 + k * 16 + t + 1]
        lng0 = lambda j: prmT[:, 0, 96 + j:97 + j]
        lnb0 = lambda j: prmT[:, 1, j:j + 1]
        hgn = prmT[:, 2, 32:33]

        if mode == "A":
            for h in range(16):
                P.op("pool", lambda e, h=h: e.memset(S[h][0][:], 0.0), writes=[S[h][0]])
        if mode == "B":
            with ExitStack() as es:
                sacc = P.tile(es, "sacc", [128, 16, 128])
                sld = P.tile(es, "sld", [128, 16, 128])
                dsl = P.tile(es, "dsl", [128, 3, 16])
                P.dma("sp", dsl[:], sl_D[:].rearrange("a p h -> p a h"), reads=[sl_D], writes=[dsl])
                P.op("act", lambda e: e.activation(out=dsl[:], in_=dsl[:], func=AF.Exp), reads=[dsl], writes=[dsl])
                P.op("pool", lambda e: e.memset(sacc[:], 0.0), writes=[sacc])
                for a in range(3):
                    P.dma("sp", sld[:], sl_S[a], reads=[sl_S], writes=[sld])
                    P.op("dve", lambda e, a=a: e.tensor_tensor(out=sacc[:], in0=sacc[:], in1=dsl[:, a, :].unsqueeze(2).broadcast_to([128, 16, 128]), op=ALU.mult),
                         reads=[sacc, dsl], writes=[sacc])
                    P.op("dve", lambda e: e.tensor_tensor(out=sacc[:], in0=sacc[:], in1=sld[:], op=ALU.add), reads=[sacc, sld], writes=[sacc])
                for h in range(16):
                    P.op("pool", lambda e, h=h: e.tensor_copy(S[h][0][:], sacc[:, h, :]), reads=[sacc], writes=[S[h][0]])
                P.barrier()

        def phase_out(t0, y_d, wo, res_ap, lng, lnb, out_ap, odt):
            with ExitStack() as es:
                yT = P.tile(es, "yT", [128, KT, TB], F32R)
                wb = [P.tile(es, f"wbB{i}", [128, KT, 128], F32R) for i in range(3)]
                xr = [P.tile(es, f"xr{i}", [128, TB], F32R) for i in range(2)]
                hp = [P.tile(es, f"hp{i}", [128, TB]) for i in range(2)]
                sq = [P.tile(es, f"sq{i}", [128, TB]) for i in range(2)]
                ho = [P.tile(es, f"ho{i}", [128, TB], odt) for i in range(2)]
                mean = P.tile(es, "mean", [128, TB])
                rstd = P.tile(es, "rstd", [128, TB])
                nmr = P.tile(es, "nmr", [128, TB])
                pA = [P.ptile(es, f"pA{i}", [128, TB]) for i in range(2)]
                pSum = P.ptile(es, "pSum", [128, TB])
                pSq = P.ptile(es, "pSq", [128, TB])
                for k4 in range(4):
                    P.dma("sp", yT[:, 8 * k4:8 * k4 + 8, :], y_d[8 * k4:8 * k4 + 8].rearrange("k p t -> p k t"), reads=[], writes=[yT])
                for j in range(KT):
                    w = wb[j % 3]
                    P.dma("sp", w[:], wo[j], reads=[], writes=[w])
                    x_ = xr[j % 2]
                    P.dma("sp", x_[:], res_ap(j), reads=[], writes=[x_])
                    ps = pA[j % 2]
                    for k in range(KT):
                        P.op("pe", lambda e, k=k: e.matmul(ps[:], w[:, k, :], yT[:, k, :], start=(k == 0), stop=(k == KT - 1)), reads=[w, yT], writes=[ps])
                    h_ = hp[j % 2]
                    P.op("dve", lambda e: e.scalar_tensor_tensor(out=h_[:], in0=x_[:].bitcast(F32), scalar=float(ALPHA), in1=ps[:], op0=ALU.mult, op1=ALU.add),
                         reads=[x_, ps], writes=[h_])
                    s_ = sq[j % 2]
                    P.op("act", lambda e: e.activation(out=s_[:], in_=h_[:], func=AF.Square), reads=[h_], writes=[s_])
                    P.op("pe", lambda e: e.matmul(pSum[:], ones[:], h_[:], start=(j == 0), stop=(j == KT - 1)), reads=[ones, h_], writes=[pSum])
                    P.op("pe", lambda e: e.matmul(pSq[:], ones[:], s_[:], start=(j == 0), stop=(j == KT - 1)), reads=[ones, s_], writes=[pSq])
                    P.dma("act", hp_d[j], h_[:], reads=[h_], writes=[Buf("hp_d")])
                P.barrier()
                P.op("act", lambda e: e.mul(mean[:], pSum[:], 1.0 / D), reads=[pSum], writes=[mean])
                P.op("dve", lambda e: e.tensor_tensor(out=nmr[:], in0=mean[:], in1=mean[:], op=ALU.mult), reads=[mean], writes=[nmr])
                P.op("dve", lambda e: e.scalar_tensor_tensor(out=rstd[:], in0=pSq[:], scalar=1.0 / D, in1=nmr[:], op0=ALU.mult, op1=ALU.subtract), reads=[pSq, nmr], writes=[rstd])
                P.op("act", lambda e: e.activation(out=rstd[:], in_=rstd[:], func=AF.Ln, bias=epsl[:, 0:1]), reads=[rstd, epsl], writes=[rstd])
                P.op("act", lambda e: e.activation(out=rstd[:], in_=rstd[:], func=AF.Exp, scale=-0.5), reads=[rstd], writes=[rstd])
                P.op("dve", lambda e: e.scalar_tensor_tensor(out=nmr[:], in0=mean[:], scalar=-1.0, in1=rstd[:], op0=ALU.mult, op1=ALU.mult), reads=[mean, rstd], writes=[nmr])
                for j in range(KT):
                    h_ = hp[j % 2]
                    P.dma("sp", h_[:], hp_d[j], reads=[], writes=[h_])
                    s_ = sq[j % 2]
                    P.op("dve", lambda e: e.tensor_tensor(out=s_[:], in0=h_[:], in1=rstd[:], op=ALU.mult), reads=[h_, rstd], writes=[s_])
                    P.op("pool", lambda e: e.tensor_tensor(out=s_[:], in0=s_[:], in1=nmr[:], op=ALU.add), reads=[s_, nmr], writes=[s_])
                    o_ = ho[j % 2]
                    P.op("dve", lambda e: e.tensor_scalar(o_[:], s_[:], lng(j), lnb(j), ALU.mult, ALU.add), reads=[s_, prmT], writes=[o_])
                    P.dma("pool", out_ap(j), o_[:], reads=[o_], writes=[Buf("h1T")])
                P.barrier()


        def l0_block(blk, full):
            t0 = blk * TB
            with ExitStack() as es:
                xT = P.tile(es, "xT", [128, KT, TB], F32R)
                wb = [P.tile(es, f"wb{i}", [128, KT, 128], F32R) for i in range(3)]
                wi = [0]
                tmp = [P.tile(es, f"tmp{i}", [128, TB]) for i in range(14)]
                qraw = P.tile(es, "qraw", [128, TB])
                czc = P.tile(es, "czc", [128, TB])
                cab = P.tile(es, "cab", [128, TB])
                csg = P.tile(es, "csg", [128, TB])
                hf2 = [P.tile(es, f"hf{i}", [128, TB]) for i in range(2)]
                hv2 = [P.tile(es, f"hv{i}", [128, TB]) for i in range(2)]
                hsg = P.tile(es, "hsg", [128, TB])
                zbuf = P.tile(es, "zbuf", [128, TB + 2])
                ytile = [P.tile(es, f"yt{i}", [128, TB], F32R) for i in range(2)]
                yi = [0]
                kv = P.tile(es, "kv", [128, 4, 256])
                AT = P.tile(es, "AT", [128, 4, 128])
                xh = P.tile(es, "xh", [128, KT, 2])
                pp = [P.ptile(es, f"pp{i}", [128, TB]) for i in range(4)]
                pT = P.ptile(es, "pT", [128, 2, 256])
                pS = P.ptile(es, "pS", [128, 4, 128])
                pO = P.ptile(es, "pO", [128, TB])
                pU = P.ptile(es, "pU", [128, 128])

                for k4 in range(4):
                    P.dma("sp", xT[:, 8 * k4:8 * k4 + 8, :], xT_in[:, 8 * k4:8 * k4 + 8, t0:t0 + TB], reads=[xT_in], writes=[xT])
                if blk == 0 and full:
                    P.dma("sp", xh[:], xhT_in[:], reads=[xhT_in], writes=[xh])

                def loadw(chunk):
                    w = wb[wi[0] % 3]
                    wi[0] += 1
                    P.dma("sp", w[:], w0[WMAP(chunk)], reads=[w0], writes=[w])
                    return w

                def proj(w, ps):
                    for k in range(KT):
                        P.op("pe", lambda e, k=k: e.matmul(ps[:], w[:, k, :], xT[:, k, :], start=(k == 0), stop=(k == KT - 1)),
                             reads=[w, xT], writes=[ps])

                def store_y(ysb, etile):
                    P.dma("pool", yT_d[etile], ysb[:], reads=[ysb], writes=[Buf("yT_d")])

                def conv_tile(i):
                    wc = loadw(16 + i)
                    proj(wc, pp[0])
                    if blk == 0:
                        for k in range(KT):
                            P.op("pe", lambda e, k=k: e.matmul(pU[:, 0:2], wc[:, k, :].bitcast(F32), xh[:, k, :], start=(k == 0), stop=(k == KT - 1)),
                                 reads=[wc, xh], writes=[pU])
                    wh = loadw(32 + i)
                    proj(wh, pp[1])
                    if blk == 0:
                        for k in range(KT):
                            P.op("pe", lambda e, k=k: e.matmul(pU[:, 2:4], wh[:, k, :].bitcast(F32), xh[:, k, :], start=(k == 0), stop=(k == KT - 1)),
                                 reads=[wh, xh], writes=[pU])
                    wbb = loadw(0 + i)
                    proj(wbb, pp[2])
                    wg = loadw(96 + i)
                    proj(wg, pp[3])
                    zc, c1, c2, ya, sg, abs_ = czc, tmp[1], tmp[2], tmp[3], csg, cab
                    P.op("act", lambda e: e.copy(zc[:], pp[0][:]), reads=[pp[0]], writes=[zc])
                    P.op("dve", lambda e: e.tensor_tensor(out=zbuf[:, 2:TB + 2], in0=zc[:], in1=pp[1][:], op=ALU.mult), reads=[zc, pp[1]], writes=[zbuf])
                    P.op("act", lambda e: e.copy(abs_[:], pp[2][:]), reads=[pp[2]], writes=[abs_])
                    P.op("act", lambda e: e.activation(out=sg[:], in_=pp[3][:], func=AF.Silu), reads=[pp[3]], writes=[sg])
                    if blk == 0:
                        P.op("act", lambda e: e.copy(zc[:, 0:2], pU[:, 0:2]), reads=[pU], writes=[zc])
                        P.op("dve", lambda e: e.tensor_tensor(out=zbuf[:, 0:2], in0=zc[:, 0:2], in1=pU[:, 2:4], op=ALU.mult), reads=[zc, pU, zbuf], writes=[zbuf])
                    else:
                        P.op("pool", lambda e: e.tensor_copy(zbuf[:, 0:2], zhalo[:, i, :]), reads=[zhalo, zbuf], writes=[zbuf])
                    P.op("pool", lambda e: e.tensor_copy(zhalo[:, i, :], zbuf[:, TB:TB + 2]), reads=[zbuf, zhalo], writes=[zhalo])
                    yield
                    P.op("pool", lambda e: e.tensor_scalar(c1[:], zbuf[:, 0:TB], convw(0, i), None, ALU.mult), reads=[zbuf, prmT], writes=[c1])
                    P.op("dve", lambda e: e.scalar_tensor_tensor(out=c2[:], in0=zbuf[:, 1:TB + 1], scalar=convw(1, i), in1=c1[:], op0=ALU.mult, op1=ALU.add),
                         reads=[zbuf, c1, prmT], writes=[c2])
                    P.op("dve", lambda e: e.scalar_tensor_tensor(out=c1[:], in0=zbuf[:, 2:TB + 2], scalar=convw(2, i), in1=c2[:], op0=ALU.mult, op1=ALU.add),
                         reads=[zbuf, c2, prmT], writes=[c1])
                    P.op("dve", lambda e: e.tensor_tensor(out=ya[:], in0=c1[:], in1=abs_[:], op=ALU.mult), reads=[c1, abs_], writes=[ya])
                    y = ytile[yi[0] % 2]
                    yi[0] += 1
                    P.op("dve", lambda e: e.tensor_tensor(out=y[:], in0=ya[:], in1=sg[:], op=ALU.mult), reads=[ya, sg], writes=[y])
                    store_y(y, i)

                def hgrn_head(h):
                    fT, lf, kk, bb, eb, enb, dd, qin, kin, kdec, vT, osq, osb, sg = tmp[:14]
                    fT, vT, sg = hf2[h % 2], hv2[h % 2], hsg
                    wf = loadw(64 + h)
                    proj(wf, pp[0])
                    wv = loadw(80 + h)
                    proj(wv, pp[1])
                    if full:
                        wq = loadw(48 + h)
                        proj(wq, pp[2])
                        wg = loadw(112 + h)
                        proj(wg, pp[3])
                    P.op("act", lambda e: e.activation(out=fT[:], in_=pp[0][:], func=AF.Sigmoid), reads=[pp[0]], writes=[fT])
                    P.op("act", lambda e: e.copy(vT[:], pp[1][:]), reads=[pp[1]], writes=[vT])
                    if full:
                        P.op("act", lambda e: e.copy(qraw[:], pp[2][:]), reads=[pp[2]], writes=[qraw])
                        P.op("act", lambda e: e.activation(out=sg[:], in_=pp[3][:], func=AF.Silu), reads=[pp[3]], writes=[sg])
                    yield
                    P.op("dve", lambda e: e.tensor_scalar(fT[:], fT[:], omlb[:, h:h + 1], lbc[:, h:h + 1], ALU.mult, ALU.add), reads=[fT, omlb, lbc], writes=[fT])
                    P.op("act", lambda e: e.activation(out=lf[:], in_=fT[:], func=AF.Ln), reads=[fT], writes=[lf])
                    P.op("pool", lambda e: e.tensor_scalar(kk[:], fT[:], -1.0, 1.0, ALU.mult, ALU.add), reads=[fT], writes=[kk])
                    P.op("dve", lambda e: e.tensor_tensor_scan(out=bb[:], data0=cmask[:], data1=lf[:], initial=0.0, op0=ALU.mult, op1=ALU.add),
                         reads=[cmask, lf], writes=[bb])
                    bv = bb[:].rearrange("p (n c) -> p n c", c=32)
                    P.op("pool", lambda e: e.tensor_tensor(out=dd[:].rearrange("p (n c) -> p n c", c=32), in0=bv[:, :, 31:32].broadcast_to([128, 16, 32]),
                                                           in1=bv, op=ALU.subtract), reads=[bb], writes=[dd])
                    P.op("act", lambda e: e.activation(out=eb[:], in_=bb[:], func=AF.Exp), reads=[bb], writes=[eb])
                    P.op("act", lambda e: e.activation(out=dd[:], in_=dd[:], func=AF.Exp), reads=[dd], writes=[dd])
                    P.op("pool", lambda e: e.tensor_tensor(out=kdec[:], in0=kk[:], in1=dd[:], op=ALU.mult), reads=[kk, dd], writes=[kdec])
                    P.op("dve", lambda e: e.reduce_sum(out=osq[:, 0:1], in_=lf[:], axis=AX.X), reads=[lf], writes=[osq])
                    P.op("dve", lambda e: e.tensor_tensor(out=lsum[:, h:h + 1], in0=lsum[:, h:h + 1], in1=osq[:, 0:1], op=ALU.add), reads=[osq, lsum], writes=[lsum])
                    if full:
                        P.op("act", lambda e: e.activation(out=enb[:], in_=bb[:], func=AF.Exp, scale=-1.0), reads=[bb], writes=[enb])
                        P.op("dve", lambda e: e.tensor_tensor(out=qin[:], in0=eb[:], in1=qraw[:], op=ALU.mult), reads=[eb, qraw], writes=[qin])
                        P.op("pool", lambda e: e.tensor_tensor(out=kin[:], in0=kk[:], in1=enb[:], op=ALU.mult), reads=[kk, enb], writes=[kin])
                    for half in range(2):
                        for t2 in range(2):
                            tt = half * 2 + t2
                            P.op("pe", lambda e, tt=tt, t2=t2: e.transpose(pT[:, t2, 0:128], kdec[:, tt * 128:(tt + 1) * 128], ident[:]), reads=[kdec, ident], writes=[pT])
                            P.op("pe", lambda e, tt=tt, t2=t2: e.transpose(pT[:, t2, 128:256], vT[:, tt * 128:(tt + 1) * 128], ident[:]), reads=[vT, ident], writes=[pT])
                        P.op("act", lambda e, half=half: e.copy(kv[:, 2 * half:2 * half + 2, :], pT[:]), reads=[pT], writes=[kv])
                    if full:
                        for tt in range(4):
                            P.op("pe", lambda e, tt=tt: e.matmul(pS[:, tt, :], kin[:, tt * 128:(tt + 1) * 128], qin[:, tt * 128:(tt + 1) * 128], start=True, stop=True),
                                 reads=[kin, qin], writes=[pS])
                        P.op("dve", lambda e: e.tensor_tensor(out=AT[:], in0=pS[:], in1=maskT[:].unsqueeze(1).broadcast_to([128, 4, 128]), op=ALU.mult),
                             reads=[pS, maskT], writes=[AT])
                    for tt in range(4):
                        if full:
                            P.op("pe", lambda e, tt=tt: e.matmul(pO[:, tt * 128:(tt + 1) * 128], kv[:, tt, 128:256], AT[:, tt, :], start=True, stop=False),
                                 reads=[kv, AT], writes=[pO])
                        for n in range(4):
                            c0 = tt * 128 + n * 32
                            Sold = S[h][Spp[h]]
                            Snew = S[h][1 - Spp[h]]
                            if full:
                                P.op("pe", lambda e, c0=c0, Sold=Sold: e.matmul(pO[:, c0:c0 + 32], Sold[:], qin[:, c0:c0 + 32], start=False, stop=True),
                                     reads=[Sold, qin], writes=[pO])
                            P.op("pe", lambda e, tt=tt, n=n: e.matmul(pU[:], kv[32 * n:32 * n + 32, tt, 0:128], kv[32 * n:32 * n + 32, tt, 128:256], start=True, stop=True,
                                                                     tile_position=(32 * n, 0)),
                                 reads=[kv], writes=[pU])
                            P.op("dve", lambda e, c0=c0, Sold=Sold, Snew=Snew: e.scalar_tensor_tensor(out=Snew[:], in0=Sold[:], scalar=eb[:, c0 + 31:c0 + 32], in1=pU[:],
                                                                                                       op0=ALU.mult, op1=ALU.add),
                                 reads=[Sold, eb, pU], writes=[Snew])
                            Spp[h] = 1 - Spp[h]
                    if full:
                        P.op("act", lambda e: e.activation(out=osq[:], in_=pO[:], func=AF.Square), reads=[pO], writes=[osq])
                        P.op("dve", lambda e: e.tensor_copy(osb[:], pO[:]), reads=[pO], writes=[osb])
                        pR = pS
                        P.op("pe", lambda e: e.matmul(pR[:].rearrange("p a b -> p (a b)"), ones[:], osq[:], start=True, stop=True), reads=[ones, osq], writes=[pR])
                        rs = kk
                        P.op("act", lambda e: e.activation(out=rs[:], in_=pR[:].rearrange("p a b -> p (a b)"), func=AF.Ln, scale=1.0 / 128.0, bias=epsr[:, 0:1]), reads=[pR, epsr], writes=[rs])
                        P.op("act", lambda e: e.activation(out=rs[:], in_=rs[:], func=AF.Exp, scale=-0.5), reads=[rs], writes=[rs])
                        P.op("dve", lambda e: e.scalar_tensor_tensor(out=osb[:], in0=osb[:], scalar=hgn, in1=rs[:], op0=ALU.mult, op1=ALU.mult), reads=[osb, rs, prmT], writes=[osb])
                        y = ytile[yi[0] % 2]
                        yi[0] += 1
                        P.op("dve", lambda e: e.tensor_tensor(out=y[:], in0=osb[:], in1=sg[:], op=ALU.mult), reads=[osb, sg], writes=[y])
                        store_y(y, 16 + h)

                gens = []
                for i in range(16):
                    gens.append(hgrn_head(i))
                    if full:
                        gens.append(conv_tile(i))
                prev = None
                for gen in gens:
                    next(gen)
                    if prev is not None:
                        for _ in prev:
                            pass
                    prev = gen
                for _ in prev:
                    pass
                P.barrier()

            if not full:
                return
            phase_out(t0, yT_d, wo0, lambda j: xT_in[:, j, t0:t0 + TB], lng0, lnb0, lambda j: h1T_out[:, j, t0:t0 + TB], F32R)

        epsr = P.tile(g, "epsr", [128, 1])
        epsl = P.tile(g, "epsl", [128, 1])
        P.op("pool", lambda e: e.memset(epsr[:], RMS_EPS), writes=[epsr])
        P.op("pool", lambda e: e.memset(epsl[:], LN_EPS), writes=[epsl])

        lng1 = lambda j: prmT[:, 1, 32 + j:33 + j]
        lnb1 = lambda j: prmT[:, 1, 64 + j:65 + j]
        dsk = lambda j: prmT[:, 1, 96 + j:97 + j]
        bgl = lambda j: prmT[:, 2, j:j + 1]

        if mode == "A":
            for blk in range(NB):
                l0_block(blk, False)
            for h in range(16):
                P.dma("sp", sloc_out[:, h, :], S[h][Spp[h]][:], reads=[S[h][Spp[h]]], writes=[sloc_out])
            P.dma("sp", dtot_out[:], lsum[:], reads=[lsum], writes=[dtot_out])
        if mode == "B":
            for blk in range(NB):
                l0_block(blk, True)

        if mode in ("B", "C"):
            full1 = mode == "C"
            src_h1 = h1T_in if full1 else h1T_out
            PI = float(np.pi)
            NL = LOGNT + 1
            lam_s = P.tile(g, "lam_s", [128, 3, 128])
            pwr = P.tile(g, "pwr", [128, NL, 128])
            pwi = P.tile(g, "pwi", [128, NL, 128])
            npwi = P.tile(g, "npwi", [128, NL, 128])
            kr = P.tile(g, "kr", [128, 128])
            ki = P.tile(g, "ki", [128, 128])
            nki = P.tile(g, "nki", [128, 128])
            nkr = P.tile(g, "nkr", [128, 128])
            car = P.tile(g, "car", [128, 128])
            cai = P.tile(g, "cai", [128, 128])
            P.dma("sp", lam_s[:], lamT[:].rearrange("a p c -> p a c"), reads=[lamT], writes=[lam_s])
            with ExitStack() as es:
                tt_ = [P.tile(es, f"s5t{i}", [128, 128]) for i in range(10)]
                dt_, ar_, Lr, Li, mag, th, kk_, sn, cs, t9 = tt_
                P.op("act", lambda e: e.activation(out=dt_[:], in_=lam_s[:, 2, :], func=AF.Exp), reads=[lam_s], writes=[dt_])
                P.op("dve", lambda e: e.tensor_scalar(ar_[:], lam_s[:, 0, :], -1e-4, None, ALU.min), reads=[lam_s], writes=[ar_])
                P.op("dve", lambda e: e.tensor_tensor(out=Lr[:], in0=ar_[:], in1=dt_[:], op=ALU.mult), reads=[ar_, dt_], writes=[Lr])
                P.op("dve", lambda e: e.tensor_tensor(out=Li[:], in0=lam_s[:, 1, :], in1=dt_[:], op=ALU.mult), reads=[lam_s, dt_], writes=[Li])
                P.op("act", lambda e: e.activation(out=mag[:], in_=Lr[:], func=AF.Exp), reads=[Lr], writes=[mag])

                def sincos(dst, shift):
                    P.op("dve", lambda e: e.tensor_scalar(th[:], Li[:], float(shift), None, ALU.add), reads=[Li], writes=[th])
                    P.op("pool", lambda e: e.memset(kk_[:], 0.0), writes=[kk_])
                    for m in range(6):
                        P.op("dve", lambda e, m=m: e.tensor_scalar(t9[:], th[:], float((2 * m + 1) * PI), None, ALU.is_ge), reads=[th], writes=[t9])
                        P.op("dve", lambda e: e.tensor_tensor(out=kk_[:], in0=kk_[:], in1=t9[:], op=ALU.add), reads=[kk_, t9], writes=[kk_])
                    P.op("dve", lambda e: e.scalar_tensor_tensor(out=th[:], in0=kk_[:], scalar=float(-2 * PI), in1=th[:], op0=ALU.mult, op1=ALU.add), reads=[kk_, th], writes=[th])
                    P.op("act", lambda e: e.activation(out=dst[:], in_=th[:], func=AF.Sin), reads=[th], writes=[dst])
                sincos(sn, 0.0)
                sincos(cs, PI / 2)
                P.op("dve", lambda e: e.tensor_tensor(out=pwr[:, 0, :], in0=mag[:], in1=cs[:], op=ALU.mult), reads=[mag, cs], writes=[pwr])
                P.op("dve", lambda e: e.tensor_tensor(out=pwi[:, 0, :], in0=mag[:], in1=sn[:], op=ALU.mult), reads=[mag, sn], writes=[pwi])
                for k in range(NL - 1):
                    P.op("dve", lambda e, k=k: e.tensor_tensor(out=dt_[:], in0=pwr[:, k, :], in1=pwr[:, k, :], op=ALU.mult), reads=[pwr], writes=[dt_])
                    P.op("dve", lambda e, k=k: e.tensor_tensor(out=mag[:], in0=pwi[:, k, :], in1=pwi[:, k, :], op=ALU.mult), reads=[pwi], writes=[mag])
                    P.op("dve", lambda e, k=k: e.tensor_tensor(out=pwr[:, k + 1, :], in0=dt_[:], in1=mag[:], op=ALU.subtract), reads=[dt_, mag, pwr], writes=[pwr])
                    P.op("dve", lambda e, k=k: e.scalar_tensor_tensor(out=pwi[:, k + 1, :], in0=pwr[:, k, :], scalar=2.0, in1=pwi[:, k, :], op0=ALU.mult, op1=ALU.mult),
                         reads=[pwr, pwi], writes=[pwi])
                P.op("dve", lambda e: e.tensor_scalar(npwi[:], pwi[:], -1.0, None, ALU.mult), reads=[pwi], writes=[npwi])
                lm1 = dt_
                P.op("dve", lambda e: e.tensor_scalar(lm1[:], pwr[:, 0, :], -1.0, None, ALU.add), reads=[pwr], writes=[lm1])
                den = mag
                P.op("dve", lambda e: e.tensor_tensor(out=den[:], in0=ar_[:], in1=ar_[:], op=ALU.mult), reads=[ar_], writes=[den])
                P.op("dve", lambda e: e.tensor_tensor(out=t9[:], in0=lam_s[:, 1, :], in1=lam_s[:, 1, :], op=ALU.mult), reads=[lam_s], writes=[t9])
                P.op("dve", lambda e: e.tensor_tensor(out=den[:], in0=den[:], in1=t9[:], op=ALU.add), reads=[den, t9], writes=[den])
                P.op("dve", lambda e: e.reciprocal(den[:], den[:]), reads=[den], writes=[den])
                P.op("dve", lambda e: e.tensor_tensor(out=th[:], in0=lm1[:], in1=ar_[:], op=ALU.mult), reads=[lm1, ar_], writes=[th])
                P.op("dve", lambda e: e.tensor_tensor(out=t9[:], in0=pwi[:, 0, :], in1=lam_s[:, 1, :], op=ALU.mult), reads=[pwi, lam_s], writes=[t9])
                P.op("dve", lambda e: e.tensor_tensor(out=th[:], in0=th[:], in1=t9[:], op=ALU.add), reads=[th, t9], writes=[th])
                P.op("dve", lambda e: e.tensor_tensor(out=kr[:], in0=th[:], in1=den[:], op=ALU.mult), reads=[th, den], writes=[kr])
                P.op("dve", lambda e: e.tensor_tensor(out=th[:], in0=pwi[:, 0, :], in1=ar_[:], op=ALU.mult), reads=[pwi, ar_], writes=[th])
                P.op("dve", lambda e: e.tensor_tensor(out=t9[:], in0=lm1[:], in1=lam_s[:, 1, :], op=ALU.mult), reads=[lm1, lam_s], writes=[t9])
                P.op("dve", lambda e: e.tensor_tensor(out=th[:], in0=th[:], in1=t9[:], op=ALU.subtract), reads=[th, t9], writes=[th])
                P.op("dve", lambda e: e.tensor_tensor(out=ki[:], in0=th[:], in1=den[:], op=ALU.mult), reads=[th, den], writes=[ki])
                P.op("dve", lambda e: e.tensor_scalar(nki[:], ki[:], -1.0, None, ALU.mult), reads=[ki], writes=[nki])
                P.op("dve", lambda e: e.tensor_scalar(nkr[:], kr[:], -1.0, None, ALU.mult), reads=[kr], writes=[nkr])
                P.op("pool", lambda e: e.memset(car[:], 0.0), writes=[car])
                P.op("pool", lambda e: e.memset(cai[:], 0.0), writes=[cai])
                if full1:
                    xsl = P.tile(es, "xsl", [128, 2, 128])
                    for a in range(3):
                        P.dma("sp", xsl[:], xs_in[a].rearrange("r p c -> p r c"), reads=[xs_in], writes=[xsl])
                        P.op("dve", lambda e: e.tensor_tensor(out=sn[:], in0=car[:], in1=pwr[:, LOGNT, :], op=ALU.mult), reads=[car, pwr], writes=[sn])
                        P.op("dve", lambda e: e.tensor_tensor(out=cs[:], in0=cai[:], in1=pwi[:, LOGNT, :], op=ALU.mult), reads=[cai, pwi], writes=[cs])
                        P.op("dve", lambda e: e.tensor_tensor(out=sn[:], in0=sn[:], in1=cs[:], op=ALU.subtract), reads=[sn, cs], writes=[sn])
                        P.op("dve", lambda e: e.tensor_tensor(out=cs[:], in0=car[:], in1=pwi[:, LOGNT, :], op=ALU.mult), reads=[car, pwi], writes=[cs])
                        P.op("dve", lambda e: e.tensor_tensor(out=th[:], in0=cai[:], in1=pwr[:, LOGNT, :], op=ALU.mult), reads=[cai, pwr], writes=[th])
                        P.op("dve", lambda e: e.tensor_tensor(out=cs[:], in0=cs[:], in1=th[:], op=ALU.add), reads=[cs, th], writes=[cs])
                        P.op("dve", lambda e: e.tensor_tensor(out=car[:], in0=sn[:], in1=xsl[:, 0, :], op=ALU.add), reads=[sn, xsl], writes=[car])
                        P.op("dve", lambda e: e.tensor_tensor(out=cai[:], in0=cs[:], in1=xsl[:, 1, :], op=ALU.add), reads=[cs, xsl], writes=[cai])
                P.barrier()

            def l1_block(blk):
                t0 = blk * TB
                with ExitStack() as es:
                    xT = P.tile(es, "xT1", [128, KT, TB], F32R)
                    wb = [P.tile(es, f"wc{i}", [128, KT, 128], F32R) for i in range(2)]
                    wi = [0]
                    uT = P.tile(es, "uT", [128, TB])
                    srs = [[P.tile(es, f"sr{q}{i}", [128, TB]) for i in range(2)] for q in range(2)]
                    sis = [[P.tile(es, f"si{q}{i}", [128, TB]) for i in range(2)] for q in range(2)]
                    t1 = P.tile(es, "t1", [128, TB])
                    t1b = P.tile(es, "t1b", [128, TB])
                    cc = P.tile(es, "cc", [128, 4])
                    gt = [P.tile(es, f"gt{i}", [128, TB], F32R) for i in range(2)]
                    sgt = [P.tile(es, f"sgt{i}", [128, TB]) for i in range(2)]
                    Bt2 = [P.tile(es, f"Bt{i}", [128, 2, 128]) for i in range(2)]
                    Ct2 = [P.tile(es, f"Ct{i}", [128, 2, 4, 32]) for i in range(2)]
                    Cp2 = [P.tile(es, f"Cp{i}", [128, 2, 32]) for i in range(2)]
                    ctmp = P.tile(es, "ctmp", [128, 32])
                    pu = P.ptile(es, "pu", [128, TB])
                    pg = P.ptile(es, "pg", [128, TB])
                    pbrs = [P.ptile(es, f"pbr{q}", [128, TB]) for q in range(2)]
                    pbis = [P.ptile(es, f"pbi{q}", [128, TB]) for q in range(2)]
                    pY = P.ptile(es, "pY", [128, TB])
                    for k4 in range(4):
                        P.dma("sp", xT[:, 8 * k4:8 * k4 + 8, :], src_h1[:, 8 * k4:8 * k4 + 8, t0:t0 + TB], reads=[], writes=[xT])

                    def loadw(chunk):
                        w = wb[wi[0] % 2]
                        wi[0] += 1
                        P.dma("sp", w[:], w1[chunk], reads=[], writes=[w])
                        return w

                    def proj(w, ps):
                        for k in range(KT):
                            P.op("pe", lambda e, k=k: e.matmul(ps[:], w[:, k, :], xT[:, k, :], start=(k == 0), stop=(k == KT - 1)), reads=[w, xT], writes=[ps])

                    for i in range(32):
                        Bsb = Bt2[i % 2]
                        P.dma("sp", Bsb[:], Bh[:, :, i, :].rearrange("a p c -> p a c"), reads=[], writes=[Bsb])
                        if full1:
                            Csb = Ct2[i % 2]
                            P.dma("sp", Csb[:], Ch[:, :, 4 * i:4 * i + 4, :].rearrange("a p i c -> p a i c"), reads=[], writes=[Csb])
                        wu = loadw(i)
                        proj(wu, pu)
                        if full1:
                            wgt = loadw(32 + i)
                            proj(wgt, pg)
                        P.op("act", lambda e: e.copy(uT[:], pu[:]), reads=[pu], writes=[uT])
                        for j in range(4):
                            pair = 4 * i + j
                            pc = slice(pair, pair + 1)
                            rows = slice(32 * j, 32 * j + 32)
                            sr, si, pbr, pbi = srs[pair % 2], sis[pair % 2], pbrs[pair % 2], pbis[pair % 2]
                            P.op("pe", lambda e: e.matmul(pbr[:], Bsb[rows, 0, :], uT[rows, :], start=True, stop=True, tile_position=(32 * j, 0)), reads=[Bsb, uT], writes=[pbr])
                            P.op("pe", lambda e: e.matmul(pbi[:], Bsb[rows, 1, :], uT[rows, :], start=True, stop=True, tile_position=(32 * j, 0)), reads=[Bsb, uT], writes=[pbi])
                            a, b_ = 0, 1
                            P.op("act", lambda e: e.copy(sr[a][:], pbr[:]), reads=[pbr], writes=[sr[a]])
                            P.op("act", lambda e: e.copy(si[a][:], pbi[:]), reads=[pbi], writes=[si[a]])
                            if full1:
                                Cp = Cp2[pair % 2]
                                P.op("dve", lambda e: e.tensor_scalar(ctmp[:], Csb[:, 0, j, :], kr[:, pc], None, ALU.mult), reads=[Csb, kr], writes=[ctmp])
                                P.op("dve", lambda e: e.scalar_tensor_tensor(out=Cp[:, 0, :], in0=Csb[:, 1, j, :], scalar=nki[:, pc], in1=ctmp[:], op0=ALU.mult, op1=ALU.add), reads=[Csb, nki, ctmp], writes=[Cp])
                                P.op("dve", lambda e: e.tensor_scalar(ctmp[:], Csb[:, 0, j, :], nki[:, pc], None, ALU.mult), reads=[Csb, nki], writes=[ctmp])
                                P.op("dve", lambda e: e.scalar_tensor_tensor(out=Cp[:, 1, :], in0=Csb[:, 1, j, :], scalar=nkr[:, pc], in1=ctmp[:], op0=ALU.mult, op1=ALU.add), reads=[Csb, nkr, ctmp], writes=[Cp])
                            P.op("dve", lambda e: e.tensor_tensor(out=cc[:, 0:1], in0=car[:, pc], in1=pwr[:, 0, pc], op=ALU.mult), reads=[car, pwr], writes=[cc])
                            P.op("dve", lambda e: e.scalar_tensor_tensor(out=cc[:, 0:1], in0=cai[:, pc], scalar=npwi[:, 0, pc], in1=cc[:, 0:1], op0=ALU.mult, op1=ALU.add), reads=[cai, npwi, cc], writes=[cc])
                            P.op("dve", lambda e: e.tensor_tensor(out=cc[:, 1:2], in0=cai[:, pc], in1=pwr[:, 0, pc], op=ALU.mult), reads=[cai, pwr, cc], writes=[cc])
                            P.op("dve", lambda e: e.scalar_tensor_tensor(out=cc[:, 1:2], in0=car[:, pc], scalar=pwi[:, 0, pc], in1=cc[:, 1:2], op0=ALU.mult, op1=ALU.add), reads=[car, pwi, cc], writes=[cc])
                            P.op("dve", lambda e: e.tensor_tensor(out=sr[a][:, 0:1], in0=sr[a][:, 0:1], in1=cc[:, 0:1], op=ALU.add), reads=[sr[a], cc], writes=[sr[a]])
                            P.op("dve", lambda e: e.tensor_tensor(out=si[a][:, 0:1], in0=si[a][:, 0:1], in1=cc[:, 1:2], op=ALU.add), reads=[si[a], cc], writes=[si[a]])
                            if full1:
                                X_r0, X_i0 = sr[a], si[a]
                                t2 = t1b

                                def bk(k, src, dst, n):
                                    P.op("dve", lambda e: e.scalar_tensor_tensor(out=t1[:, 0:n], in0=X_r0[:, src], scalar=pwr[:, k, pc], in1=X_r0[:, dst], op0=ALU.mult, op1=ALU.add),
                                         reads=[X_r0, pwr, t1], writes=[t1], nosame=True)
                                    P.op("dve", lambda e: e.scalar_tensor_tensor(out=t2[:, 0:n], in0=X_i0[:, src], scalar=pwr[:, k, pc], in1=X_i0[:, dst], op0=ALU.mult, op1=ALU.add),
                                         reads=[X_i0, pwr, t2], writes=[t2], nosame=True)
                                    P.op("dve", lambda e: e.scalar_tensor_tensor(out=X_r0[:, dst], in0=X_i0[:, src], scalar=npwi[:, k, pc], in1=t1[:, 0:n], op0=ALU.mult, op1=ALU.add),
                                         reads=[X_i0, npwi, t1, X_r0], writes=[X_r0], nosame=True)
                                    P.op("dve", lambda e: e.scalar_tensor_tensor(out=X_i0[:, dst], in0=X_r0[:, src], scalar=pwi[:, k, pc], in1=t2[:, 0:n], op0=ALU.mult, op1=ALU.add),
                                         reads=[X_r0, pwi, t2, X_i0], writes=[X_i0], nosame=True)

                                for k in range(9):
                                    S_ = 2 << k
                                    n = TB // S_
                                    bk(k, slice((1 << k) - 1, TB, S_), slice(S_ - 1, TB, S_), n)
                                for k in range(7, -1, -1):
                                    S_ = 2 << k
                                    n = TB // S_ - 1
                                    bk(k, slice(S_ - 1, min(TB, S_ - 1 + S_ * n), S_), slice(S_ + (1 << k) - 1, min(TB, S_ + (1 << k) - 1 + S_ * n), S_), n)
                                lastc = slice(TB - 1, TB)
                            else:
                                n = TB
                                for k in range(9):
                                    n //= 2
                                    A_r, A_i, B_r, B_i = sr[a], si[a], sr[b_], si[b_]
                                    ev = slice(0, 2 * n, 2)
                                    od = slice(1, 2 * n, 2)
                                    P.op("dve", lambda e: e.scalar_tensor_tensor(out=t1[:, 0:n], in0=A_r[:, ev], scalar=pwr[:, k, pc], in1=A_r[:, od], op0=ALU.mult, op1=ALU.add),
                                         reads=[A_r, pwr], writes=[t1], nosame=True)
                                    P.op("dve", lambda e: e.scalar_tensor_tensor(out=B_r[:, 0:n], in0=A_i[:, ev], scalar=npwi[:, k, pc], in1=t1[:, 0:n], op0=ALU.mult, op1=ALU.add),
                                         reads=[A_i, npwi, t1, B_r], writes=[B_r], nosame=True)
                                    P.op("dve", lambda e: e.scalar_tensor_tensor(out=t1[:, 0:n], in0=A_i[:, ev], scalar=pwr[:, k, pc], in1=A_i[:, od], op0=ALU.mult, op1=ALU.add),
                                         reads=[A_i, pwr, t1], writes=[t1], nosame=True)
                                    P.op("dve", lambda e: e.scalar_tensor_tensor(out=B_i[:, 0:n], in0=A_r[:, ev], scalar=pwi[:, k, pc], in1=t1[:, 0:n], op0=ALU.mult, op1=ALU.add),
                                         reads=[A_r, pwi, t1, B_i], writes=[B_i], nosame=True)
                                    a, b_ = b_, a
                                lastc = slice(0, 1)
                            X_r, X_i = sr[a], si[a]
                            P.op("dve", lambda e: e.tensor_copy(car[:, pc], X_r[:, lastc]), reads=[X_r, car], writes=[car], nosame=True)
                            P.op("dve", lambda e: e.tensor_copy(cai[:, pc], X_i[:, lastc]), reads=[X_i, cai], writes=[cai], nosame=True)
                            if full1:
                                P.op("pe", lambda e: e.matmul(pY[rows, :], Cp[:, 0, :], X_r[:], start=True, stop=False, tile_position=(0, 32 * j)), reads=[Cp, X_r], writes=[pY])
                                P.op("pe", lambda e: e.matmul(pY[rows, :], Cp[:, 1, :], X_i[:], start=False, stop=True, tile_position=(0, 32 * j)), reads=[Cp, X_i], writes=[pY])
                        if full1:
                            g_ = gt[i % 2]
                            s_ = sgt[i % 2]
                            P.op("dve", lambda e: e.scalar_tensor_tensor(out=t1[:], in0=uT[:], scalar=dsk(i), in1=pY[:], op0=ALU.mult, op1=ALU.add), reads=[uT, prmT, pY], writes=[t1])
                            P.op("act", lambda e: e.activation(out=g_[:], in_=t1[:], func=AF.Gelu_apprx_tanh), reads=[t1], writes=[g_])
                            P.op("act", lambda e: e.activation(out=s_[:], in_=pg[:], func=AF.Silu), reads=[pg], writes=[s_])
                            P.dma("pool", yT_d[i], g_[:], reads=[g_], writes=[Buf("yT_d")])
                            P.dma("pool", sg_d[i], s_[:], reads=[s_], writes=[Buf("sg_d")])
                    P.barrier()
                if not full1:
                    return
                with ExitStack() as es:
                    gT = P.tile(es, "gT", [128, KT, TB], F32R)
                    wb = [P.tile(es, f"wd{i}", [128, KT, 128], F32R) for i in range(3)]
                    sgl = [P.tile(es, f"sgl{i}", [128, TB]) for i in range(2)]
                    zz = [P.tile(es, f"zz{i}", [128, TB]) for i in range(2)]
                    y2 = [P.tile(es, f"y2{i}", [128, TB], F32R) for i in range(2)]
                    pz = [P.ptile(es, f"pz{i}", [128, TB]) for i in range(2)]
                    for k4 in range(4):
                        P.dma("sp", gT[:, 8 * k4:8 * k4 + 8, :], yT_d[8 * k4:8 * k4 + 8].rearrange("k p t -> p k t"), reads=[], writes=[gT])
                    for j in range(KT):
                        w = wb[j % 3]
                        P.dma("sp", w[:], wg1[j], reads=[], writes=[w])
                        sl_ = sgl[j % 2]
                        P.dma("sp", sl_[:], sg_d[j], reads=[], writes=[sl_])
                        ps = pz[j % 2]
                        for k in range(KT):
                            P.op("pe", lambda e, k=k: e.matmul(ps[:], w[:, k, :], gT[:, k, :], start=(k == 0), stop=(k == KT - 1)), reads=[w, gT], writes=[ps])
                        z_ = zz[j % 2]
                        P.op("act", lambda e: e.activation(out=z_[:], in_=ps[:], func=AF.Sigmoid, bias=bgl(j)), reads=[ps, prmT], writes=[z_])
                        P.op("dve", lambda e: e.tensor_tensor(out=z_[:], in0=z_[:], in1=gT[:, j, :].bitcast(F32), op=ALU.mult), reads=[z_, gT], writes=[z_])
                        o_ = y2[j % 2]
                        P.op("dve", lambda e: e.tensor_tensor(out=o_[:], in0=z_[:], in1=sl_[:], op=ALU.mult), reads=[z_, sl_], writes=[o_])
                        P.dma("pool", y2_d[j], o_[:], reads=[o_], writes=[Buf("y2_d")])
                    P.barrier()
                phase_out(t0, y2_d, wo1, lambda j: h1T_in[:, j, t0:t0 + TB], lng1, lnb1, lambda j: outT[:, j, t0:t0 + TB], F32)

            for blk in range(NB):
                l1_block(blk)
            if not full1:
                P.dma("sp", xloc_out[0], car[:], reads=[car], writes=[xloc_out])
                P.dma("sp", xloc_out[1], cai[:], reads=[cai], writes=[xloc_out])
        P.barrier()
    print("instructions:", P.ninst, flush=True)
    return nc


def _chunks(w, idx_tiles):
    E = w.shape[1]
    wt = w.reshape(KT, 128, E // 128, 128)
    wt = wt[:, :, idx_tiles, :]
    return np.ascontiguousarray(wt.transpose(2, 1, 0, 3))


def _scan_layout(a):
    return np.ascontiguousarray(a.reshape(128, 2, 64).transpose(1, 2, 0).reshape(128, 128))


def _prep_s5(inp):
    lam = np.stack([_scan_layout(inp["od_lam_re"][0]), _scan_layout(inp["od_lam_im"][0]),
                    _scan_layout(np.broadcast_to(inp["od_log_step"][0][:, None], (256, 64)))]).astype(np.float32)
    Bh = np.zeros((2, 128, 32, 128), np.float32)
    Ch = np.zeros((2, 128, 128, 32), np.float32)
    for r, (bk, ck) in enumerate([("od_b_re", "od_c_re"), ("od_b_im", "od_c_im")]):
        b = inp[bk][0]
        c = inp[ck][0]
        for gq in range(256):
            i, gl = gq // 8, gq % 8
            g2 = gl % 2
            Bh[r, gl * 16:(gl + 1) * 16, i, g2 * 64:(g2 + 1) * 64] = b[gq].T
            Ch[r, g2 * 64:(g2 + 1) * 64, gq // 2, g2 * 16:(g2 + 1) * 16] = c[gq].T
    return lam, Bh, Ch


def run_pipeline(inp, x, NT):
    ids = list(range(NCORES))
    prm = pack_params(inp)
    xTs, xhTs = [], []
    for c in range(NCORES):
        b, sg = c // 4, c % 4
        xc = x[b, sg * NT:(sg + 1) * NT]
        xTs.append(np.ascontiguousarray(xc.reshape(NT, 32, 128).transpose(2, 1, 0)))
        xh = np.zeros((2, D), np.float32) if sg == 0 else x[b, sg * NT - 2:sg * NT]
        xhTs.append(np.ascontiguousarray(xh.reshape(2, 32, 128).transpose(2, 1, 0)))
    w0 = _chunks(inp["ev_w_in"][0], list(range(128)))
    ncA = build(NT, "A")
    wA = np.ascontiguousarray(w0[64:96])
    rA = run_bass_kernel_spmd(ncA, [{"prm": prm, "xT": xTs[c], "w0": wA} for c in ids], core_ids=ids).results
    wo0 = _chunks(inp["ev_w_out"][0], list(range(32)))
    w1 = _chunks(inp["od_w_in"][0], list(range(64)))
    lam, Bh, Ch = _prep_s5(inp)
    mapsB = []
    for c in ids:
        sg = c % 4
        slS = np.zeros((3, 128, 16, 128), np.float32)
        slD = np.zeros((3, 128, 16), np.float32)
        for j in range(sg):
            slS[3 - sg + j] = rA[c - sg + j]["sloc"]
            slD[3 - sg + j] = rA[c - sg + j]["dtot"]
        mapsB.append({"prm": prm, "xT": xTs[c], "xhT": xhTs[c], "w0": w0, "wo0": wo0, "sl_S": slS, "sl_D": slD,
                      "w1": np.ascontiguousarray(w1[:32]), "lamT": lam, "Bh": Bh, "Ch": Ch})
    ncB = build(NT, "B")
    rB = run_bass_kernel_spmd(ncB, mapsB, core_ids=ids).results
    del w0, wo0, mapsB
    wg1 = _chunks(inp["od_w_glu"][0], list(range(32)))
    wo1 = _chunks(inp["od_w_out"][0], list(range(32)))
    mapsC = []
    for c in ids:
        sg = c % 4
        xs = np.zeros((3, 2, 128, 128), np.float32)
        for j in range(sg):
            xs[3 - sg + j] = rB[c - sg + j]["xloc"]
        mapsC.append({"prm": prm, "h1T": rB[c]["h1T"], "w1": w1, "wg1": wg1, "wo1": wo1, "xs": xs, "lamT": lam, "Bh": Bh, "Ch": Ch})
    ncC = build(NT, "C")
    rC = run_bass_kernel_spmd(ncC, mapsC, core_ids=ids).results
    out = np.zeros((2, 4 * NT, D), np.float32)
    for c in ids:
        b, sg = c // 4, c % 4
        out[b, sg * NT:(sg + 1) * NT] = rC[c]["outT"].transpose(2, 1, 0).reshape(NT, D)
    return out, rB


def kernel(**inp):
    inp = {k: np.asarray(v) for k, v in inp.items()}
    out, _ = run_pipeline(inp, inp["x"], 2048)
    return out


def pack_params(inp):
    prm = np.zeros((3, 128, 128), np.float32)
    prm[0, 0:48] = inp["hg_lb_logits"].reshape(48, 128)
    prm[0, 48:96] = inp["ev_conv_w"][0].reshape(48, 128)
    prm[0, 96:128] = inp["ev_ln_g"][0].reshape(32, 128)
    prm[1, 0:32] = inp["ev_ln_b"][0].reshape(32, 128)
    prm[1, 32:64] = inp["od_ln_g"][0].reshape(32, 128)
    prm[1, 64:96] = inp["od_ln_b"][0].reshape(32, 128)
    prm[1, 96:128] = inp["od_d"][0].reshape(32, 128)
    prm[2, 0:32] = inp["od_b_glu"][0].reshape(32, 128)
    prm[2, 32] = inp["ev_hg_norm"][0]
    return prm
```

```python
from contextlib import ExitStack
import numpy as np
import concourse.bass as bass
import concourse.mybir as mybir
from concourse.bass_utils import run_bass_kernel_spmd

F32 = mybir.dt.float32
F32R = mybir.dt.float32r
AF = mybir.ActivationFunctionType
ALU = mybir.AluOpType
AX = mybir.AxisListType

D = 4096
KT = 32
TB = 512
NCORES = 8
ALPHA = 4.0 ** 0.25
LN_EPS = 1e-5
RMS_EPS = 1e-6


class Buf:
    def __init__(self, name):
        self.name = name
        self.w = None
        self.r = {}
        self.dsem = None
        self.dcnt = 0


class T:
    def __init__(self, t, name):
        self.t = t
        self.b = Buf(name)

    def __getitem__(self, k):
        return self.t[k]


class Prog:
    def __init__(self, nc):
        self.nc = nc
        self.eng = {"pe": nc.tensor, "act": nc.scalar, "dve": nc.vector, "pool": nc.gpsimd, "sp": nc.sync}
        self.sem = {e: nc.alloc_semaphore("s_" + e) for e in ["pe", "act", "dve", "pool"]}
        self.cnt = {e: 0 for e in self.sem}
        self.waited = {}
        self.dsems = {}
        self.all_dsems = []
        self.ninst = 0

    def tile(self, es, name, shape, dtype=F32):
        self.uid = getattr(self, "uid", 0) + 1
        t = es.enter_context(self.nc.sbuf_tensor(f"{name}_{self.uid}", shape, dtype))
        return T(t, name)

    def ptile(self, es, name, shape, dtype=F32):
        self.uid = getattr(self, "uid", 0) + 1
        t = es.enter_context(self.nc.psum_tensor(f"{name}_{self.uid}", shape, dtype))
        return T(t, name)

    def dram(self, name, shape, dtype=F32, kind="Internal"):
        t = self.nc.dram_tensor(name, shape, dtype, kind=kind)
        return T(t.ap(), name)

    def _wait(self, e, tok):
        sem, val = tok
        key = (e, sem.num)
        if self.waited.get(key, 0) >= val:
            return
        self.waited[key] = val
        self.eng[e].wait_ge(sem, val)

    def _deps(self, e, reads, writes, nosame=False):
        toks = []
        for b in reads:
            if b.w:
                toks.append(b.w)
        for b in writes:
            if b.w:
                toks.append(b.w)
            toks.extend(b.r.values())
        for t in toks:
            if t[0] is self.sem.get(e) and (e == "pe" or nosame):
                continue
            self._wait(e, t)

    def _post(self, tok, reads, writes):
        for b in writes:
            b.w = tok
            b.r = {}
        for b in reads:
            if b not in writes:
                b.r[tok[0].num] = tok

    def op(self, e, fn, reads=(), writes=(), nosame=False):
        reads = [x.b if isinstance(x, T) else x for x in reads]
        writes = [x.b if isinstance(x, T) else x for x in writes]
        self._deps(e, reads, writes, nosame)
        inst = fn(self.eng[e])
        self.cnt[e] += 1
        self.ninst += 1
        inst.then_inc(self.sem[e], 1)
        tok = (self.sem[e], self.cnt[e])
        self._post(tok, reads, writes)

    def dma(self, e, out_ap, in_ap, reads=(), writes=()):
        reads = [x.b if isinstance(x, T) else x for x in reads]
        writes = [x.b if isinstance(x, T) else x for x in writes]
        self._deps(e, reads, writes)
        sb = writes[0]
        if sb.dsem is None:
            if sb.name not in self.dsems:
                self.dsems[sb.name] = [self.nc.alloc_semaphore("d_" + sb.name), 0]
                self.all_dsems.append(self.dsems[sb.name])
            sb.dsem = self.dsems[sb.name]
        sb.dsem[1] += 16
        self.eng[e].dma_start(out=out_ap, in_=in_ap).then_inc(sb.dsem[0], 16)
        self.ninst += 1
        tok = (sb.dsem[0], sb.dsem[1])
        self._post(tok, reads, writes)

    def barrier(self):
        for e in ["pe", "act", "dve", "pool", "sp"]:
            for e2 in self.sem:
                if e2 != e and self.cnt[e2] > 0:
                    self._wait(e, (self.sem[e2], self.cnt[e2]))
            for ds in self.all_dsems:
                if ds[1] > 0:
                    self._wait(e, (ds[0], ds[1]))


def build(NT, mode):
    NB = NT // TB
    nc = bass.Bass("TRN2", target_bir_lowering=False)
    nc.dge_precook = False
    P = Prog(nc)
    din = lambda name, shape, dt=F32: T(nc.dram_tensor(name, shape, dt, kind="ExternalInput").ap(), name)
    dout = lambda name, shape, dt=F32: T(nc.dram_tensor(name, shape, dt, kind="ExternalOutput").ap(), name)

    LOGNT = NT.bit_length() - 1
    prm = din("prm", [3, 128, 128], F32)
    if mode in ("A", "B"):
        xT_in = din("xT", [128, KT, NT], F32R)
    if mode == "A":
        w0 = din("w0", [32, 128, KT, 128], F32R)
        sloc_out = dout("sloc", [128, 16, 128], F32)
        dtot_out = dout("dtot", [128, 16], F32)
        WMAP = lambda c: c - 64
    if mode == "B":
        xhT_in = din("xhT", [128, KT, 2], F32)
        w0 = din("w0", [128, 128, KT, 128], F32R)
        wo0 = din("wo0", [32, 128, KT, 128], F32R)
        sl_S = din("sl_S", [3, 128, 16, 128], F32)
        sl_D = din("sl_D", [3, 128, 16], F32)
        h1T_out = dout("h1T", [128, KT, NT], F32R)
        xloc_out = dout("xloc", [2, 128, 128], F32)
        WMAP = lambda c: c
    if mode == "C":
        h1T_in = din("h1T", [128, KT, NT], F32R)
        wg1 = din("wg1", [32, 128, KT, 128], F32R)
        wo1 = din("wo1", [32, 128, KT, 128], F32R)
        xs_in = din("xs", [3, 2, 128, 128], F32)
        outT = dout("outT", [128, KT, NT], F32)
    if mode in ("B", "C"):
        w1 = din("w1", [64 if mode == "C" else 32, 128, KT, 128], F32R)
        lamT = din("lamT", [3, 128, 128], F32)
        Bh = din("Bh", [2, 128, 32, 128], F32)
        Ch = din("Ch", [2, 128, 128, 32], F32)

    yT_d = P.dram("yT_d", [32, 128, TB], F32R)
    hp_d = P.dram("hp_d", [32, 128, TB], F32)
    sg_d = P.dram("sg_d", [32, 128, TB], F32)
    y2_d = P.dram("y2_d", [32, 128, TB], F32R)

    with ExitStack() as g:
        ident = P.tile(g, "ident", [128, 128])
        ones = P.tile(g, "ones", [128, 128])
        cmask = P.tile(g, "cmask", [128, TB])
        maskT = P.tile(g, "maskT", [128, 128])
        prmT = P.tile(g, "prmT", [128, 3, 128])
        prm_s = P.tile(g, "prm_s", [128, 3, 128])
        lbc = P.tile(g, "lbc", [128, 16])
        omlb = P.tile(g, "omlb", [128, 16])
        SD = 128 if mode != "C" else 1
        S = [[P.tile(g, f"S{h}_{i}", [128, SD]) for i in range(2)] for h in range(16)]
        Spp = [0] * 16
        zhalo = P.tile(g, "zhalo", [128, 16, 2])
        lsum = P.tile(g, "lsum", [128, 16])

        P.op("pool", lambda e: e.memset(ident[:], 0.0), writes=[ident])
        P.op("pool", lambda e: e.affine_select(out=ident[:], in_=ident[:], pattern=[[-1, 128]], compare_op=ALU.not_equal,
                                               fill=1.0, base=0, channel_multiplier=1), reads=[ident], writes=[ident])
        P.op("pool", lambda e: e.memset(ones[:], 1.0), writes=[ones])
        P.op("pool", lambda e: e.memset(cmask[:], 1.0), writes=[cmask])
        P.op("pool", lambda e: e.memset(cmask[:].rearrange("p (n c) -> p n c", c=32)[:, :, 0:1], 0.0), reads=[cmask], writes=[cmask])
        P.op("pool", lambda e: e.memset(maskT[:], 1.0), writes=[maskT])
        P.op("pool", lambda e: e.affine_select(out=maskT[:], in_=maskT[:], pattern=[[1, 128]], compare_op=ALU.is_ge,
                                               fill=0.0, base=0, channel_multiplier=-1), reads=[maskT], writes=[maskT])
        for i in range(3):
            P.op("pool", lambda e, i=i: e.memset(maskT[32 * i:32 * i + 32, 32 * (i + 1):128], 0.0), reads=[maskT], writes=[maskT])
        P.op("pool", lambda e: e.memset(lsum[:], 0.0), writes=[lsum])

        P.dma("sp", prm_s[:], prm[:].rearrange("a r c -> r a c"), reads=[prm], writes=[prm_s])
        with ExitStack() as es:
            pt = P.ptile(es, "pt0", [128, 3, 128])
            for a in range(3):
                P.op("pe", lambda e, a=a: e.transpose(pt[:, a, :], prm_s[:, a, :], ident[:]), reads=[prm_s, ident], writes=[pt])
            P.op("dve", lambda e: e.tensor_copy(prmT[:], pt[:]), reads=[pt], writes=[prmT])
            ex = P.tile(es, "ex", [128, 48])
            sm = P.tile(es, "sm", [128, 16])
            P.op("act", lambda e: e.activation(out=ex[:], in_=prmT[:, 0, 0:48], func=AF.Exp), reads=[prmT], writes=[ex])
            P.op("dve", lambda e: e.tensor_tensor(out=sm[:], in0=ex[:, 0:16], in1=ex[:, 16:32], op=ALU.add), reads=[ex], writes=[sm])
            P.op("dve", lambda e: e.tensor_tensor(out=sm[:], in0=sm[:], in1=ex[:, 32:48], op=ALU.add), reads=[ex, sm], writes=[sm])
            P.op("dve", lambda e: e.reciprocal(sm[:], sm[:]), reads=[sm], writes=[sm])
            P.op("dve", lambda e: e.tensor_tensor(out=lbc[:], in0=ex[:, 0:16], in1=sm[:], op=ALU.mult), reads=[ex, sm], writes=[lbc])
            P.op("dve", lambda e: e.tensor_scalar(omlb[:], lbc[:], -1.0, 1.0, ALU.mult, ALU.add), reads=[lbc], writes=[omlb])
            P.barrier()
        convw = lambda k, t: prmT[:, 0, 48 + k * 16 + t:48 + k * 16 + t + 1]
        lng0 = lambda j: prmT[:, 0, 96 + j:97 + j]
        lnb0 = lambda j: prmT[:, 1, j:j + 1]
        hgn = prmT[:, 2, 32:33]

        if mode == "A":
            for h in range(16):
                P.op("pool", lambda e, h=h: e.memset(S[h][0][:], 0.0), writes=[S[h][0]])
        if mode == "B":
            with ExitStack() as es:
                sacc = P.tile(es, "sacc", [128, 16, 128])
                sld = P.tile(es, "sld", [128, 16, 128])
                dsl = P.tile(es, "dsl", [128, 3, 16])
                P.dma("sp", dsl[:], sl_D[:].rearrange("a p h -> p a h"), reads=[sl_D], writes=[dsl])
                P.op("act", lambda e: e.activation(out=dsl[:], in_=dsl[:], func=AF.Exp), reads=[dsl], writes=[dsl])
                P.op("pool", lambda e: e.memset(sacc[:], 0.0), writes=[sacc])
                for a in range(3):
                    P.dma("sp", sld[:], sl_S[a], reads=[sl_S], writes=[sld])
                    P.op("dve", lambda e, a=a: e.tensor_tensor(out=sacc[:], in0=sacc[:], in1=dsl[:, a, :].unsqueeze(2).broadcast_to([128, 16, 128]), op=ALU.mult),
                         reads=[sacc, dsl], writes=[sacc])
                    P.op("dve", lambda e: e.tensor_tensor(out=sacc[:], in0=sacc[:], in1=sld[:], op=ALU.add), reads=[sacc, sld], writes=[sacc])
                for h in range(16):
                    P.op("pool", lambda e, h=h: e.tensor_copy(S[h][0][:], sacc[:, h, :]), reads=[sacc], writes=[S[h][0]])
                P.barrier()

        def phase_out(t0, y_d, wo, res_ap, lng, lnb, out_ap, odt):
            with ExitStack() as es:
                yT = P.tile(es, "yT", [128, KT, TB], F32R)
                wb = [P.tile(es, f"wbB{i}", [128, KT, 128], F32R) for i in range(3)]
                xr = [P.tile(es, f"xr{i}", [128, TB], F32R) for i in range(2)]
                hp = [P.tile(es, f"hp{i}", [128, TB]) for i in range(2)]
                sq = [P.tile(es, f"sq{i}", [128, TB]) for i in range(2)]
                ho = [P.tile(es, f"ho{i}", [128, TB], odt) for i in range(2)]
                mean = P.tile(es, "mean", [128, TB])
                rstd = P.tile(es, "rstd", [128, TB])
                nmr = P.tile(es, "nmr", [128, TB])
                pA = [P.ptile(es, f"pA{i}", [128, TB]) for i in range(2)]
                pSum = P.ptile(es, "pSum", [128, TB])
                pSq = P.ptile(es, "pSq", [128, TB])
                for k4 in range(4):
                    P.dma("sp", yT[:, 8 * k4:8 * k4 + 8, :], y_d[8 * k4:8 * k4 + 8].rearrange("k p t -> p k t"), reads=[], writes=[yT])
                for j in range(KT):
                    w = wb[j % 3]
                    P.dma("sp", w[:], wo[j], reads=[], writes=[w])
                    x_ = xr[j % 2]
                    P.dma("sp", x_[:], res_ap(j), reads=[], writes=[x_])
                    ps = pA[j % 2]
                    for k in range(KT):
                        P.op("pe", lambda e, k=k: e.matmul(ps[:], w[:, k, :], yT[:, k, :], start=(k == 0), stop=(k == KT - 1)), reads=[w, yT], writes=[ps])
                    h_ = hp[j % 2]
                    P.op("dve", lambda e: e.scalar_tensor_tensor(out=h_[:], in0=x_[:].bitcast(F32), scalar=float(ALPHA), in1=ps[:], op0=ALU.mult, op1=ALU.add),
                         reads=[x_, ps], writes=[h_])
                    s_ = sq[j % 2]
                    P.op("act", lambda e: e.activation(out=s_[:], in_=h_[:], func=AF.Square), reads=[h_], writes=[s_])
                    P.op("pe", lambda e: e.matmul(pSum[:], ones[:], h_[:], start=(j == 0), stop=(j == KT - 1)), reads=[ones, h_], writes=[pSum])
                    P.op("pe", lambda e: e.matmul(pSq[:], ones[:], s_[:], start=(j == 0), stop=(j == KT - 1)), reads=[ones, s_], writes=[pSq])
                    P.dma("act", hp_d[j], h_[:], reads=[h_], writes=[Buf("hp_d")])
                P.barrier()
                P.op("act", lambda e: e.mul(mean[:], pSum[:], 1.0 / D), reads=[pSum], writes=[mean])
                P.op("dve", lambda e: e.tensor_tensor(out=nmr[:], in0=mean[:], in1=mean[:], op=ALU.mult), reads=[mean], writes=[nmr])
                P.op("dve", lambda e: e.scalar_tensor_tensor(out=rstd[:], in0=pSq[:], scalar=1.0 / D, in1=nmr[:], op0=ALU.mult, op1=ALU.subtract), reads=[pSq, nmr], writes=[rstd])
                P.op("act", lambda e: e.activation(out=rstd[:], in_=rstd[:], func=AF.Ln, bias=epsl[:, 0:1]), reads=[rstd, epsl], writes=[rstd])
                P.op("act", lambda e: e.activation(out=rstd[:], in_=rstd[:], func=AF.Exp, scale=-0.5), reads=[rstd], writes=[rstd])
                P.op("dve", lambda e: e.scalar_tensor_tensor(out=nmr[:], in0=mean[:], scalar=-1.0, in1=rstd[:], op0=ALU.mult, op1=ALU.mult), reads=[mean, rstd], writes=[nmr])
                for j in range(KT):
                    h_ = hp[j % 2]
                    P.dma("sp", h_[:], hp_d[j], reads=[], writes=[h_])
                    s_ = sq[j % 2]
                    P.op("dve", lambda e: e.tensor_tensor(out=s_[:], in0=h_[:], in1=rstd[:], op=ALU.mult), reads=[h_, rstd], writes=[s_])
                    P.op("pool", lambda e: e.tensor_tensor(out=s_[:], in0=s_[:], in1=nmr[:], op=ALU.add), reads=[s_, nmr], writes=[s_])
                    o_ = ho[j % 2]
                    P.op("dve", lambda e: e.tensor_scalar(o_[:], s_[:], lng(j), lnb(j), ALU.mult, ALU.add), reads=[s_, prmT], writes=[o_])
                    P.dma("pool", out_ap(j), o_[:], reads=[o_], writes=[Buf("h1T")])
                P.barrier()


        def l0_block(blk, full):
            t0 = blk * TB
            with ExitStack() as es:
                xT = P.tile(es, "xT", [128, KT, TB], F32R)
                wb = [P.tile(es, f"wb{i}", [128, KT, 128], F32R) for i in range(3)]
                wi = [0]
                tmp = [P.tile(es, f"tmp{i}", [128, TB]) for i in range(14)]
                qraw = P.tile(es, "qraw", [128, TB])
                czc = P.tile(es, "czc", [128, TB])
                cab = P.tile(es, "cab", [128, TB])
                csg = P.tile(es, "csg", [128, TB])
                hf2 = [P.tile(es, f"hf{i}", [128, TB]) for i in range(2)]
                hv2 = [P.tile(es, f"hv{i}", [128, TB]) for i in range(2)]
                hsg = P.tile(es, "hsg", [128, TB])
                zbuf = P.tile(es, "zbuf", [128, TB + 2])
                ytile = [P.tile(es, f"yt{i}", [128, TB], F32R) for i in range(2)]
                yi = [0]
                kv = P.tile(es, "kv", [128, 4, 256])
                AT = P.tile(es, "AT", [128, 4, 128])
                xh = P.tile(es, "xh", [128, KT, 2])
                pp = [P.ptile(es, f"pp{i}", [128, TB]) for i in range(4)]
                pT = P.ptile(es, "pT", [128, 2, 256])
                pS = P.ptile(es, "pS", [128, 4, 128])
                pO = P.ptile(es, "pO", [128, TB])
                pU = P.ptile(es, "pU", [128, 128])

                for k4 in range(4):
                    P.dma("sp", xT[:, 8 * k4:8 * k4 + 8, :], xT_in[:, 8 * k4:8 * k4 + 8, t0:t0 + TB], reads=[xT_in], writes=[xT])
                if blk == 0 and full:
                    P.dma("sp", xh[:], xhT_in[:], reads=[xhT_in], writes=[xh])

                def loadw(chunk):
                    w = wb[wi[0] % 3]
                    wi[0] += 1
                    P.dma("sp", w[:], w0[WMAP(chunk)], reads=[w0], writes=[w])
                    return w

                def proj(w, ps):
                    for k in range(KT):
                        P.op("pe", lambda e, k=k: e.matmul(ps[:], w[:, k, :], xT[:, k, :], start=(k == 0), stop=(k == KT - 1)),
                             reads=[w, xT], writes=[ps])

                def store_y(ysb, etile):
                    P.dma("pool", yT_d[etile], ysb[:], reads=[ysb], writes=[Buf("yT_d")])

                def conv_tile(i):
                    wc = loadw(16 + i)
                    proj(wc, pp[0])
                    if blk == 0:
                        for k in range(KT):
                            P.op("pe", lambda e, k=k: e.matmul(pU[:, 0:2], wc[:, k, :].bitcast(F32), xh[:, k, :], start=(k == 0), stop=(k == KT - 1)),
                                 reads=[wc, xh], writes=[pU])
                    wh = loadw(32 + i)
                    proj(wh, pp[1])
                    if blk == 0:
                        for k in range(KT):
                            P.op("pe", lambda e, k=k: e.matmul(pU[:, 2:4], wh[:, k, :].bitcast(F32), xh[:, k, :], start=(k == 0), stop=(k == KT - 1)),
                                 reads=[wh, xh], writes=[pU])
                    wbb = loadw(0 + i)
                    proj(wbb, pp[2])
                    wg = loadw(96 + i)
                    proj(wg, pp[3])
                    zc, c1, c2, ya, sg, abs_ = czc, tmp[1], tmp[2], tmp[3], csg, cab
                    P.op("act", lambda e: e.copy(zc[:], pp[0][:]), reads=[pp[0]], writes=[zc])
                    P.op("dve", lambda e: e.tensor_tensor(out=zbuf[:, 2:TB + 2], in0=zc[:], in1=pp[1][:], op=ALU.mult), reads=[zc, pp[1]], writes=[zbuf])
                    P.op("act", lambda e: e.copy(abs_[:], pp[2][:]), reads=[pp[2]], writes=[abs_])
                    P.op("act", lambda e: e.activation(out=sg[:], in_=pp[3][:], func=AF.Silu), reads=[pp[3]], writes=[sg])
                    if blk == 0:
                        P.op("act", lambda e: e.copy(zc[:, 0:2], pU[:, 0:2]), reads=[pU], writes=[zc])
                        P.op("dve", lambda e: e.tensor_tensor(out=zbuf[:, 0:2], in0=zc[:, 0:2], in1=pU[:, 2:4], op=ALU.mult), reads=[zc, pU, zbuf], writes=[zbuf])
                    else:
                        P.op("pool", lambda e: e.tensor_copy(zbuf[:, 0:2], zhalo[:, i, :]), reads=[zhalo, zbuf], writes=[zbuf])
                    P.op("pool", lambda e: e.tensor_copy(zhalo[:, i, :], zbuf[:, TB:TB + 2]), reads=[zbuf, zhalo], writes=[zhalo])
                    yield
                    P.op("pool", lambda e: e.tensor_scalar(c1[:], zbuf[:, 0:TB], convw(0, i), None, ALU.mult), reads=[zbuf, prmT], writes=[c1])
                    P.op("dve", lambda e: e.scalar_tensor_tensor(out=c2[:], in0=zbuf[:, 1:TB + 1], scalar=convw(1, i), in1=c1[:], op0=ALU.mult, op1=ALU.add),
                         reads=[zbuf, c1, prmT], writes=[c2])
                    P.op("dve", lambda e: e.scalar_tensor_tensor(out=c1[:], in0=zbuf[:, 2:TB + 2], scalar=convw(2, i), in1=c2[:], op0=ALU.mult, op1=ALU.add),
                         reads=[zbuf, c2, prmT], writes=[c1])
                    P.op("dve", lambda e: e.tensor_tensor(out=ya[:], in0=c1[:], in1=abs_[:], op=ALU.mult), reads=[c1, abs_], writes=[ya])
                    y = ytile[yi[0] % 2]
                    yi[0] += 1
                    P.op("dve", lambda e: e.tensor_tensor(out=y[:], in0=ya[:], in1=sg[:], op=ALU.mult), reads=[ya, sg], writes=[y])
                    store_y(y, i)

                def hgrn_head(h):
                    fT, lf, kk, bb, eb, enb, dd, qin, kin, kdec, vT, osq, osb, sg = tmp[:14]
                    fT, vT, sg = hf2[h % 2], hv2[h % 2], hsg
                    wf = loadw(64 + h)
                    proj(wf, pp[0])
                    wv = loadw(80 + h)
                    proj(wv, pp[1])
                    if full:
                        wq = loadw(48 + h)
                        proj(wq, pp[2])
                        wg = loadw(112 + h)
                        proj(wg, pp[3])
                    P.op("act", lambda e: e.activation(out=fT[:], in_=pp[0][:], func=AF.Sigmoid), reads=[pp[0]], writes=[fT])
                    P.op("act", lambda e: e.copy(vT[:], pp[1][:]), reads=[pp[1]], writes=[vT])
                    if full:
                        P.op("act", lambda e: e.copy(qraw[:], pp[2][:]), reads=[pp[2]], writes=[qraw])
                        P.op("act", lambda e: e.activation(out=sg[:], in_=pp[3][:], func=AF.Silu), reads=[pp[3]], writes=[sg])
                    yield
                    P.op("dve", lambda e: e.tensor_scalar(fT[:], fT[:], omlb[:, h:h + 1], lbc[:, h:h + 1], ALU.mult, ALU.add), reads=[fT, omlb, lbc], writes=[fT])
                    P.op("act", lambda e: e.activation(out=lf[:], in_=fT[:], func=AF.Ln), reads=[fT], writes=[lf])
                    P.op("pool", lambda e: e.tensor_scalar(kk[:], fT[:], -1.0, 1.0, ALU.mult, ALU.add), reads=[fT], writes=[kk])
                    P.op("dve", lambda e: e.tensor_tensor_scan(out=bb[:], data0=cmask[:], data1=lf[:], initial=0.0, op0=ALU.mult, op1=ALU.add),
                         reads=[cmask, lf], writes=[bb])
                    bv = bb[:].rearrange("p (n c) -> p n c", c=32)
                    P.op("pool", lambda e: e.tensor_tensor(out=dd[:].rearrange("p (n c) -> p n c", c=32), in0=bv[:, :, 31:32].broadcast_to([128, 16, 32]),
                                                           in1=bv, op=ALU.subtract), reads=[bb], writes=[dd])
                    P.op("act", lambda e: e.activation(out=eb[:], in_=bb[:], func=AF.Exp), reads=[bb], writes=[eb])
                    P.op("act", lambda e: e.activation(out=dd[:], in_=dd[:], func=AF.Exp), reads=[dd], writes=[dd])
                    P.op("pool", lambda e: e.tensor_tensor(out=kdec[:], in0=kk[:], in1=dd[:], op=ALU.mult), reads=[kk, dd], writes=[kdec])
                    P.op("dve", lambda e: e.reduce_sum(out=osq[:, 0:1], in_=lf[:], axis=AX.X), reads=[lf], writes=[osq])
                    P.op("dve", lambda e: e.tensor_tensor(out=lsum[:, h:h + 1], in0=lsum[:, h:h + 1], in1=osq[:, 0:1], op=ALU.add), reads=[osq, lsum], writes=[lsum])
                    if full:
                        P.op("act", lambda e: e.activation(out=enb[:], in_=bb[:], func=AF.Exp, scale=-1.0), reads=[bb], writes=[enb])
                        P.op("dve", lambda e: e.tensor_tensor(out=qin[:], in0=eb[:], in1=qraw[:], op=ALU.mult), reads=[eb, qraw], writes=[qin])
                        P.op("pool", lambda e: e.tensor_tensor(out=kin[:], in0=kk[:], in1=enb[:], op=ALU.mult), reads=[kk, enb], writes=[kin])
                    for half in range(2):
                        for t2 in range(2):
                            tt = half * 2 + t2
                            P.op("pe", lambda e, tt=tt, t2=t2: e.transpose(pT[:, t2, 0:128], kdec[:, tt * 128:(tt + 1) * 128], ident[:]), reads=[kdec, ident], writes=[pT])
                            P.op("pe", lambda e, tt=tt, t2=t2: e.transpose(pT[:, t2, 128:256], vT[:, tt * 128:(tt + 1) * 128], ident[:]), reads=[vT, ident], writes=[pT])
                        P.op("act", lambda e, half=half: e.copy(kv[:, 2 * half:2 * half + 2, :], pT[:]), reads=[pT], writes=[kv])
                    if full:
                        for tt in range(4):
                            P.op("pe", lambda e, tt=tt: e.matmul(pS[:, tt, :], kin[:, tt * 128:(tt + 1) * 128], qin[:, tt * 128:(tt + 1) * 128], start=True, stop=True),
                                 reads=[kin, qin], writes=[pS])
                        P.op("dve", lambda e: e.tensor_tensor(out=AT[:], in0=pS[:], in1=maskT[:].unsqueeze(1).broadcast_to([128, 4, 128]), op=ALU.mult),
                             reads=[pS, maskT], writes=[AT])
                    for tt in range(4):
                        if full:
                            P.op("pe", lambda e, tt=tt: e.matmul(pO[:, tt * 128:(tt + 1) * 128], kv[:, tt, 128:256], AT[:, tt, :], start=True, stop=False),
                                 reads=[kv, AT], writes=[pO])
                        for n in range(4):
                            c0 = tt * 128 + n * 32
                            Sold = S[h][Spp[h]]
                            Snew = S[h][1 - Spp[h]]
                            if full:
                                P.op("pe", lambda e, c0=c0, Sold=Sold: e.matmul(pO[:, c0:c0 + 32], Sold[:], qin[:, c0:c0 + 32], start=False, stop=True),
                                     reads=[Sold, qin], writes=[pO])
                            P.op("pe", lambda e, tt=tt, n=n: e.matmul(pU[:], kv[32 * n:32 * n + 32, tt, 0:128], kv[32 * n:32 * n + 32, tt, 128:256], start=True, stop=True,
                                                                     tile_position=(32 * n, 0)),
                                 reads=[kv], writes=[pU])
                            P.op("dve", lambda e, c0=c0, Sold=Sold, Snew=Snew: e.scalar_tensor_tensor(out=Snew[:], in0=Sold[:], scalar=eb[:, c0 + 31:c0 + 32], in1=pU[:],
                                                                                                       op0=ALU.mult, op1=ALU.add),
                                 reads=[Sold, eb, pU], writes=[Snew])
                            Spp[h] = 1 - Spp[h]
                    if full:
                        P.op("act", lambda e: e.activation(out=osq[:], in_=pO[:], func=AF.Square), reads=[pO], writes=[osq])
                        P.op("dve", lambda e: e.tensor_copy(osb[:], pO[:]), reads=[pO], writes=[osb])
                        pR = pS
                        P.op("pe", lambda e: e.matmul(pR[:].rearrange("p a b -> p (a b)"), ones[:], osq[:], start=True, stop=True), reads=[ones, osq], writes=[pR])
                        rs = kk
                        P.op("act", lambda e: e.activation(out=rs[:], in_=pR[:].rearrange("p a b -> p (a b)"), func=AF.Ln, scale=1.0 / 128.0, bias=epsr[:, 0:1]), reads=[pR, epsr], writes=[rs])
                        P.op("act", lambda e: e.activation(out=rs[:], in_=rs[:], func=AF.Exp, scale=-0.5), reads=[rs], writes=[rs])
                        P.op("dve", lambda e: e.scalar_tensor_tensor(out=osb[:], in0=osb[:], scalar=hgn, in1=rs[:], op0=ALU.mult, op1=ALU.mult), reads=[osb, rs, prmT], writes=[osb])
                        y = ytile[yi[0] % 2]
                        yi[0] += 1
                        P.op("dve", lambda e: e.tensor_tensor(out=y[:], in0=osb[:], in1=sg[:], op=ALU.mult), reads=[osb, sg], writes=[y])
                        store_y(y, 16 + h)

                gens = []
                for i in range(16):
                    gens.append(hgrn_head(i))
                    if full:
                        gens.append(conv_tile(i))
                prev = None
                for gen in gens:
                    next(gen)
                    if prev is not None:
                        for _ in prev:
                            pass
                    prev = gen
                for _ in prev:
                    pass
                P.barrier()

            if not full:
                return
            phase_out(t0, yT_d, wo0, lambda j: xT_in[:, j, t0:t0 + TB], lng0, lnb0, lambda j: h1T_out[:, j, t0:t0 + TB], F32R)

        epsr = P.tile(g, "epsr", [128, 1])
        epsl = P.tile(g, "epsl", [128, 1])
        P.op("pool", lambda e: e.memset(epsr[:], RMS_EPS), writes=[epsr])
        P.op("pool", lambda e: e.memset(epsl[:], LN_EPS), writes=[epsl])

        lng1 = lambda j: prmT[:, 1, 32 + j:33 + j]
        lnb1 = lambda j: prmT[:, 1, 64 + j:65 + j]
        dsk = lambda j: prmT[:, 1, 96 + j:97 + j]
        bgl = lambda j: prmT[:, 2, j:j + 1]

        if mode == "A":
            for blk in range(NB):
                l0_block(blk, False)
            for h in range(16):
                P.dma("sp", sloc_out[:, h, :], S[h][Spp[h]][:], reads=[S[h][Spp[h]]], writes=[sloc_out])
            P.dma("sp", dtot_out[:], lsum[:], reads=[lsum], writes=[dtot_out])
        if mode == "B":
            for blk in range(NB):
                l0_block(blk, True)

        if mode in ("B", "C"):
            full1 = mode == "C"
            src_h1 = h1T_in if full1 else h1T_out
            PI = float(np.pi)
            NL = LOGNT + 1
            lam_s = P.tile(g, "lam_s", [128, 3, 128])
            pwr = P.tile(g, "pwr", [128, NL, 128])
            pwi = P.tile(g, "pwi", [128, NL, 128])
            npwi = P.tile(g, "npwi", [128, NL, 128])
            kr = P.tile(g, "kr", [128, 128])
            ki = P.tile(g, "ki", [128, 128])
            nki = P.tile(g, "nki", [128, 128])
            nkr = P.tile(g, "nkr", [128, 128])
            car = P.tile(g, "car", [128, 128])
            cai = P.tile(g, "cai", [128, 128])
            P.dma("sp", lam_s[:], lamT[:].rearrange("a p c -> p a c"), reads=[lamT], writes=[lam_s])
            with ExitStack() as es:
                tt_ = [P.tile(es, f"s5t{i}", [128, 128]) for i in range(10)]
                dt_, ar_, Lr, Li, mag, th, kk_, sn, cs, t9 = tt_
                P.op("act", lambda e: e.activation(out=dt_[:], in_=lam_s[:, 2, :], func=AF.Exp), reads=[lam_s], writes=[dt_])
                P.op("dve", lambda e: e.tensor_scalar(ar_[:], lam_s[:, 0, :], -1e-4, None, ALU.min), reads=[lam_s], writes=[ar_])
                P.op("dve", lambda e: e.tensor_tensor(out=Lr[:], in0=ar_[:], in1=dt_[:], op=ALU.mult), reads=[ar_, dt_], writes=[Lr])
                P.op("dve", lambda e: e.tensor_tensor(out=Li[:], in0=lam_s[:, 1, :], in1=dt_[:], op=ALU.mult), reads=[lam_s, dt_], writes=[Li])
                P.op("act", lambda e: e.activation(out=mag[:], in_=Lr[:], func=AF.Exp), reads=[Lr], writes=[mag])

                def sincos(dst, shift):
                    P.op("dve", lambda e: e.tensor_scalar(th[:], Li[:], float(shift), None, ALU.add), reads=[Li], writes=[th])
                    P.op("pool", lambda e: e.memset(kk_[:], 0.0), writes=[kk_])
                    for m in range(6):
                        P.op("dve", lambda e, m=m: e.tensor_scalar(t9[:], th[:], float((2 * m + 1) * PI), None, ALU.is_ge), reads=[th], writes=[t9])
                        P.op("dve", lambda e: e.tensor_tensor(out=kk_[:], in0=kk_[:], in1=t9[:], op=ALU.add), reads=[kk_, t9], writes=[kk_])
                    P.op("dve", lambda e: e.scalar_tensor_tensor(out=th[:], in0=kk_[:], scalar=float(-2 * PI), in1=th[:], op0=ALU.mult, op1=ALU.add), reads=[kk_, th], writes=[th])
                    P.op("act", lambda e: e.activation(out=dst[:], in_=th[:], func=AF.Sin), reads=[th], writes=[dst])
                sincos(sn, 0.0)
                sincos(cs, PI / 2)
                P.op("dve", lambda e: e.tensor_tensor(out=pwr[:, 0, :], in0=mag[:], in1=cs[:], op=ALU.mult), reads=[mag, cs], writes=[pwr])
                P.op("dve", lambda e: e.tensor_tensor(out=pwi[:, 0, :], in0=mag[:], in1=sn[:], op=ALU.mult), reads=[mag, sn], writes=[pwi])
                for k in range(NL - 1):
                    P.op("dve", lambda e, k=k: e.tensor_tensor(out=dt_[:], in0=pwr[:, k, :], in1=pwr[:, k, :], op=ALU.mult), reads=[pwr], writes=[dt_])
                    P.op("dve", lambda e, k=k: e.tensor_tensor(out=mag[:], in0=pwi[:, k, :], in1=pwi[:, k, :], op=ALU.mult), reads=[pwi], writes=[mag])
                    P.op("dve", lambda e, k=k: e.tensor_tensor(out=pwr[:, k + 1, :], in0=dt_[:], in1=mag[:], op=ALU.subtract), reads=[dt_, mag, pwr], writes=[pwr])
                    P.op("dve", lambda e, k=k: e.scalar_tensor_tensor(out=pwi[:, k + 1, :], in0=pwr[:, k, :], scalar=2.0, in1=pwi[:, k, :], op0=ALU.mult, op1=ALU.mult),
                         reads=[pwr, pwi], writes=[pwi])
                P.op("dve", lambda e: e.tensor_scalar(npwi[:], pwi[:], -1.0, None, ALU.mult), reads=[pwi], writes=[npwi])
                lm1 = dt_
                P.op("dve", lambda e: e.tensor_scalar(lm1[:], pwr[:, 0, :], -1.0, None, ALU.add), reads=[pwr], writes=[lm1])
                den = mag
                P.op("dve", lambda e: e.tensor_tensor(out=den[:], in0=ar_[:], in1=ar_[:], op=ALU.mult), reads=[ar_], writes=[den])
                P.op("dve", lambda e: e.tensor_tensor(out=t9[:], in0=lam_s[:, 1, :], in1=lam_s[:, 1, :], op=ALU.mult), reads=[lam_s], writes=[t9])
                P.op("dve", lambda e: e.tensor_tensor(out=den[:], in0=den[:], in1=t9[:], op=ALU.add), reads=[den, t9], writes=[den])
                P.op("dve", lambda e: e.reciprocal(den[:], den[:]), reads=[den], writes=[den])
                P.op("dve", lambda e: e.tensor_tensor(out=th[:], in0=lm1[:], in1=ar_[:], op=ALU.mult), reads=[lm1, ar_], writes=[th])
                P.op("dve", lambda e: e.tensor_tensor(out=t9[:], in0=pwi[:, 0, :], in1=lam_s[:, 1, :], op=ALU.mult), reads=[pwi, lam_s], writes=[t9])
                P.op("dve", lambda e: e.tensor_tensor(out=th[:], in0=th[:], in1=t9[:], op=ALU.add), reads=[th, t9], writes=[th])
                P.op("dve", lambda e: e.tensor_tensor(out=kr[:], in0=th[:], in1=den[:], op=ALU.mult), reads=[th, den], writes=[kr])
                P.op("dve", lambda e: e.tensor_tensor(out=th[:], in0=pwi[:, 0, :], in1=ar_[:], op=ALU.mult), reads=[pwi, ar_], writes=[th])
                P.op("dve", lambda e: e.tensor_tensor(out=t9[:], in0=lm1[:], in1=lam_s[:, 1, :], op=ALU.mult), reads=[lm1, lam_s], writes=[t9])
                P.op("dve", lambda e: e.tensor_tensor(out=th[:], in0=th[:], in1=t9[:], op=ALU.subtract), reads=[th, t9], writes=[th])
                P.op("dve", lambda e: e.tensor_tensor(out=ki[:], in0=th[:], in1=den[:], op=ALU.mult), reads=[th, den], writes=[ki])
                P.op("dve", lambda e: e.tensor_scalar(nki[:], ki[:], -1.0, None, ALU.mult), reads=[ki], writes=[nki])
                P.op("dve", lambda e: e.tensor_scalar(nkr[:], kr[:], -1.0, None, ALU.mult), reads=[kr], writes=[nkr])
                P.op("pool", lambda e: e.memset(car[:], 0.0), writes=[car])
                P.op("pool", lambda e: e.memset(cai[:], 0.0), writes=[cai])
                if full1:
                    xsl = P.tile(es, "xsl", [128, 2, 128])
                    for a in range(3):
                        P.dma("sp", xsl[:], xs_in[a].rearrange("r p c -> p r c"), reads=[xs_in], writes=[xsl])
                        P.op("dve", lambda e: e.tensor_tensor(out=sn[:], in0=car[:], in1=pwr[:, LOGNT, :], op=ALU.mult), reads=[car, pwr], writes=[sn])
                        P.op("dve", lambda e: e.tensor_tensor(out=cs[:], in0=cai[:], in1=pwi[:, LOGNT, :], op=ALU.mult), reads=[cai, pwi], writes=[cs])
                        P.op("dve", lambda e: e.tensor_tensor(out=sn[:], in0=sn[:], in1=cs[:], op=ALU.subtract), reads=[sn, cs], writes=[sn])
                        P.op("dve", lambda e: e.tensor_tensor(out=cs[:], in0=car[:], in1=pwi[:, LOGNT, :], op=ALU.mult), reads=[car, pwi], writes=[cs])
                        P.op("dve", lambda e: e.tensor_tensor(out=th[:], in0=cai[:], in1=pwr[:, LOGNT, :], op=ALU.mult), reads=[cai, pwr], writes=[th])
                        P.op("dve", lambda e: e.tensor_tensor(out=cs[:], in0=cs[:], in1=th[:], op=ALU.add), reads=[cs, th], writes=[cs])
                        P.op("dve", lambda e: e.tensor_tensor(out=car[:], in0=sn[:], in1=xsl[:, 0, :], op=ALU.add), reads=[sn, xsl], writes=[car])
                        P.op("dve", lambda e: e.tensor_tensor(out=cai[:], in0=cs[:], in1=xsl[:, 1, :], op=ALU.add), reads=[cs, xsl], writes=[cai])
                P.barrier()

            def l1_block(blk):
                t0 = blk * TB
                with ExitStack() as es:
                    xT = P.tile(es, "xT1", [128, KT, TB], F32R)
                    wb = [P.tile(es, f"wc{i}", [128, KT, 128], F32R) for i in range(2)]
                    wi = [0]
                    uT = P.tile(es, "uT", [128, TB])
                    uT_b = P.tile(es, "uT_b", [128, TB])
                    srs = [[P.tile(es, f"sr{q}{i}", [128, TB]) for i in range(2)] for q in range(2)]
                    sis = [[P.tile(es, f"si{q}{i}", [128, TB]) for i in range(2)] for q in range(2)]
                    t1 = P.tile(es, "t1", [128, TB])
                    t1b = P.tile(es, "t1b", [128, TB])
                    cc = P.tile(es, "cc", [128, 4])
                    gt = [P.tile(es, f"gt{i}", [128, TB], F32R) for i in range(2)]
                    sgt = [P.tile(es, f"sgt{i}", [128, TB]) for i in range(2)]
                    Bt2 = [P.tile(es, f"Bt{i}", [128, 2, 128]) for i in range(2)]
                    Ct2 = [P.tile(es, f"Ct{i}", [128, 2, 4, 32]) for i in range(2)]
                    Cp2 = [P.tile(es, f"Cp{i}", [128, 2, 32]) for i in range(2)]
                    ctmp = P.tile(es, "ctmp", [128, 32])
                    pu = P.ptile(es, "pu", [128, TB])
                    pg = P.ptile(es, "pg", [128, TB])
                    pbrs = [P.ptile(es, f"pbr{q}", [128, TB]) for q in range(2)]
                    pbis = [P.ptile(es, f"pbi{q}", [128, TB]) for q in range(2)]
                    pY = P.ptile(es, "pY", [128, TB])
                    for k4 in range(4):
                        P.dma("sp", xT[:, 8 * k4:8 * k4 + 8, :], src_h1[:, 8 * k4:8 * k4 + 8, t0:t0 + TB], reads=[], writes=[xT])

                    def loadw(chunk):
                        w = wb[wi[0] % 2]
                        wi[0] += 1
                        P.dma("sp", w[:], w1[chunk], reads=[], writes=[w])
                        return w

                    def proj(w, ps):
                        for k in range(KT):
                            P.op("pe", lambda e, k=k: e.matmul(ps[:], w[:, k, :], xT[:, k, :], start=(k == 0), stop=(k == KT - 1)), reads=[w, xT], writes=[ps])

                    uT2 = [uT, uT_b]

                    def tile(i):
                        Bsb = Bt2[i % 2]
                        uTi = uT2[i % 2]
                        P.dma("sp", Bsb[:], Bh[:, :, i, :].rearrange("a p c -> p a c"), reads=[], writes=[Bsb])
                        if full1:
                            Csb = Ct2[i % 2]
                            P.dma("sp", Csb[:], Ch[:, :, 4 * i:4 * i + 4, :].rearrange("a p i c -> p a i c"), reads=[], writes=[Csb])
                        wu = loadw(i)
                        proj(wu, pu)
                        P.op("act", lambda e: e.copy(uTi[:], pu[:]), reads=[pu], writes=[uTi])
                        if full1:
                            wgt = loadw(32 + i)
                            proj(wgt, pg)
                            s_ = sgt[i % 2]
                            P.op("act", lambda e: e.activation(out=s_[:], in_=pg[:], func=AF.Silu), reads=[pg], writes=[s_])
                        st = {}

                        def P1(j):
                            pair = 4 * i + j
                            pc = slice(pair, pair + 1)
                            rows = slice(32 * j, 32 * j + 32)
                            sr, si, pbr, pbi = srs[pair % 2], sis[pair % 2], pbrs[pair % 2], pbis[pair % 2]
                            P.op("pe", lambda e: e.matmul(pbr[:], Bsb[rows, 0, :], uTi[rows, :], start=True, stop=True, tile_position=(32 * j, 0)), reads=[Bsb, uTi], writes=[pbr])
                            P.op("pe", lambda e: e.matmul(pbi[:], Bsb[rows, 1, :], uTi[rows, :], start=True, stop=True, tile_position=(32 * j, 0)), reads=[Bsb, uTi], writes=[pbi])
                            P.op("act", lambda e: e.copy(sr[0][:], pbr[:]), reads=[pbr], writes=[sr[0]])
                            P.op("act", lambda e: e.copy(si[0][:], pbi[:]), reads=[pbi], writes=[si[0]])
                            Cp = None
                            if full1:
                                Cp = Cp2[pair % 2]
                                P.op("dve", lambda e: e.tensor_scalar(ctmp[:], Csb[:, 0, j, :], kr[:, pc], None, ALU.mult), reads=[Csb, kr], writes=[ctmp])
                                P.op("dve", lambda e: e.scalar_tensor_tensor(out=Cp[:, 0, :], in0=Csb[:, 1, j, :], scalar=nki[:, pc], in1=ctmp[:], op0=ALU.mult, op1=ALU.add), reads=[Csb, nki, ctmp], writes=[Cp])
                                P.op("dve", lambda e: e.tensor_scalar(ctmp[:], Csb[:, 0, j, :], nki[:, pc], None, ALU.mult), reads=[Csb, nki], writes=[ctmp])
                                P.op("dve", lambda e: e.scalar_tensor_tensor(out=Cp[:, 1, :], in0=Csb[:, 1, j, :], scalar=nkr[:, pc], in1=ctmp[:], op0=ALU.mult, op1=ALU.add), reads=[Csb, nkr, ctmp], writes=[Cp])
                            st[j] = (pair, pc, rows, sr, si, Cp)

                        def P2(j):
                            pair, pc, rows, sr, si, Cp = st[j]
                            a, b_ = 0, 1
                            P.op("dve", lambda e: e.tensor_tensor(out=cc[:, 0:1], in0=car[:, pc], in1=pwr[:, 0, pc], op=ALU.mult), reads=[car, pwr], writes=[cc])
                            P.op("dve", lambda e: e.scalar_tensor_tensor(out=cc[:, 0:1], in0=cai[:, pc], scalar=npwi[:, 0, pc], in1=cc[:, 0:1], op0=ALU.mult, op1=ALU.add), reads=[cai, npwi, cc], writes=[cc])
                            P.op("dve", lambda e: e.tensor_tensor(out=cc[:, 1:2], in0=cai[:, pc], in1=pwr[:, 0, pc], op=ALU.mult), reads=[cai, pwr, cc], writes=[cc])
                            P.op("dve", lambda e: e.scalar_tensor_tensor(out=cc[:, 1:2], in0=car[:, pc], scalar=pwi[:, 0, pc], in1=cc[:, 1:2], op0=ALU.mult, op1=ALU.add), reads=[car, pwi, cc], writes=[cc])
                            P.op("dve", lambda e: e.tensor_tensor(out=sr[a][:, 0:1], in0=sr[a][:, 0:1], in1=cc[:, 0:1], op=ALU.add), reads=[sr[a], cc], writes=[sr[a]])
                            P.op("dve", lambda e: e.tensor_tensor(out=si[a][:, 0:1], in0=si[a][:, 0:1], in1=cc[:, 1:2], op=ALU.add), reads=[si[a], cc], writes=[si[a]])
                            if full1:
                                X_r0, X_i0 = sr[a], si[a]
                                t2 = t1b

                                def bk(k, src, dst, n):
                                    P.op("dve", lambda e: e.scalar_tensor_tensor(out=t1[:, 0:n], in0=X_r0[:, src], scalar=pwr[:, k, pc], in1=X_r0[:, dst], op0=ALU.mult, op1=ALU.add),
                                         reads=[X_r0, pwr, t1], writes=[t1], nosame=True)
                                    P.op("dve", lambda e: e.scalar_tensor_tensor(out=t2[:, 0:n], in0=X_i0[:, src], scalar=pwr[:, k, pc], in1=X_i0[:, dst], op0=ALU.mult, op1=ALU.add),
                                         reads=[X_i0, pwr, t2], writes=[t2], nosame=True)
                                    P.op("dve", lambda e: e.scalar_tensor_tensor(out=X_r0[:, dst], in0=X_i0[:, src], scalar=npwi[:, k, pc], in1=t1[:, 0:n], op0=ALU.mult, op1=ALU.add),
                                         reads=[X_i0, npwi, t1, X_r0], writes=[X_r0], nosame=True)
                                    P.op("dve", lambda e: e.scalar_tensor_tensor(out=X_i0[:, dst], in0=X_r0[:, src], scalar=pwi[:, k, pc], in1=t2[:, 0:n], op0=ALU.mult, op1=ALU.add),
                                         reads=[X_r0, pwi, t2, X_i0], writes=[X_i0], nosame=True)

                                for k in range(9):
                                    S_ = 2 << k
                                    n = TB // S_
                                    bk(k, slice((1 << k) - 1, TB, S_), slice(S_ - 1, TB, S_), n)
                                for k in range(7, -1, -1):
                                    S_ = 2 << k
                                    n = TB // S_ - 1
                                    bk(k, slice(S_ - 1, min(TB, S_ - 1 + S_ * n), S_), slice(S_ + (1 << k) - 1, min(TB, S_ + (1 << k) - 1 + S_ * n), S_), n)
                                lastc = slice(TB - 1, TB)
                            else:
                                n = TB
                                for k in range(9):
                                    n //= 2
                                    A_r, A_i, B_r, B_i = sr[a], si[a], sr[b_], si[b_]
                                    ev = slice(0, 2 * n, 2)
                                    od = slice(1, 2 * n, 2)
                                    P.op("dve", lambda e: e.scalar_tensor_tensor(out=t1[:, 0:n], in0=A_r[:, ev], scalar=pwr[:, k, pc], in1=A_r[:, od], op0=ALU.mult, op1=ALU.add),
                                         reads=[A_r, pwr], writes=[t1], nosame=True)
                                    P.op("dve", lambda e: e.scalar_tensor_tensor(out=B_r[:, 0:n], in0=A_i[:, ev], scalar=npwi[:, k, pc], in1=t1[:, 0:n], op0=ALU.mult, op1=ALU.add),
                                         reads=[A_i, npwi, t1, B_r], writes=[B_r], nosame=True)
                                    P.op("dve", lambda e: e.scalar_tensor_tensor(out=t1[:, 0:n], in0=A_i[:, ev], scalar=pwr[:, k, pc], in1=A_i[:, od], op0=ALU.mult, op1=ALU.add),
                                         reads=[A_i, pwr, t1], writes=[t1], nosame=True)
                                    P.op("dve", lambda e: e.scalar_tensor_tensor(out=B_i[:, 0:n], in0=A_r[:, ev], scalar=pwi[:, k, pc], in1=t1[:, 0:n], op0=ALU.mult, op1=ALU.add),
                                         reads=[A_r, pwi, t1, B_i], writes=[B_i], nosame=True)
                                    a, b_ = b_, a
                                lastc = slice(0, 1)
                            X_r, X_i = sr[a], si[a]
                            P.op("dve", lambda e: e.tensor_copy(car[:, pc], X_r[:, lastc]), reads=[X_r, car], writes=[car], nosame=True)
                            P.op("dve", lambda e: e.tensor_copy(cai[:, pc], X_i[:, lastc]), reads=[X_i, cai], writes=[cai], nosame=True)
                            if full1:
                                P.op("pe", lambda e: e.matmul(pY[rows, :], Cp[:, 0, :], X_r[:], start=True, stop=False, tile_position=(0, 32 * j)), reads=[Cp, X_r], writes=[pY])
                                P.op("pe", lambda e: e.matmul(pY[rows, :], Cp[:, 1, :], X_i[:], start=False, stop=True, tile_position=(0, 32 * j)), reads=[Cp, X_i], writes=[pY])

                        P1(0)
                        yield
                        for j in range(3):
                            P1(j + 1)
                            P2(j)
                        yield
                        P2(3)
                        if full1:
                            g_ = gt[i % 2]
                            P.op("dve", lambda e: e.scalar_tensor_tensor(out=t1[:], in0=uTi[:], scalar=dsk(i), in1=pY[:], op0=ALU.mult, op1=ALU.add), reads=[uTi, prmT, pY], writes=[t1])
                            P.op("act", lambda e: e.activation(out=g_[:], in_=t1[:], func=AF.Gelu_apprx_tanh), reads=[t1], writes=[g_])
                            P.dma("pool", yT_d[i], g_[:], reads=[g_], writes=[Buf("yT_d")])
                            P.dma("pool", sg_d[i], s_[:], reads=[s_], writes=[Buf("sg_d")])

                    gcur = tile(0)
                    next(gcur)
                    for i in range(32):
                        next(gcur)
                        gnext = None
                        if i + 1 < 32:
                            gnext = tile(i + 1)
                            next(gnext)
                        for _ in gcur:
                            pass
                        gcur = gnext
                    P.barrier()
                if not full1:
                    return
                with ExitStack() as es:
                    gT = P.tile(es, "gT", [128, KT, TB], F32R)
                    wb = [P.tile(es, f"wd{i}", [128, KT, 128], F32R) for i in range(3)]
                    sgl = [P.tile(es, f"sgl{i}", [128, TB]) for i in range(2)]
                    zz = [P.tile(es, f"zz{i}", [128, TB]) for i in range(2)]
                    y2 = [P.tile(es, f"y2{i}", [128, TB], F32R) for i in range(2)]
                    pz = [P.ptile(es, f"pz{i}", [128, TB]) for i in range(2)]
                    for k4 in range(4):
                        P.dma("sp", gT[:, 8 * k4:8 * k4 + 8, :], yT_d[8 * k4:8 * k4 + 8].rearrange("k p t -> p k t"), reads=[], writes=[gT])
                    for j in range(KT):
                        w = wb[j % 3]
                        P.dma("sp", w[:], wg1[j], reads=[], writes=[w])
                        sl_ = sgl[j % 2]
                        P.dma("sp", sl_[:], sg_d[j], reads=[], writes=[sl_])
                        ps = pz[j % 2]
                        for k in range(KT):
                            P.op("pe", lambda e, k=k: e.matmul(ps[:], w[:, k, :], gT[:, k, :], start=(k == 0), stop=(k == KT - 1)), reads=[w, gT], writes=[ps])
                        z_ = zz[j % 2]
                        P.op("act", lambda e: e.activation(out=z_[:], in_=ps[:], func=AF.Sigmoid, bias=bgl(j)), reads=[ps, prmT], writes=[z_])
                        P.op("dve", lambda e: e.tensor_tensor(out=z_[:], in0=z_[:], in1=gT[:, j, :].bitcast(F32), op=ALU.mult), reads=[z_, gT], writes=[z_])
                        o_ = y2[j % 2]
                        P.op("dve", lambda e: e.tensor_tensor(out=o_[:], in0=z_[:], in1=sl_[:], op=ALU.mult), reads=[z_, sl_], writes=[o_])
                        P.dma("pool", y2_d[j], o_[:], reads=[o_], writes=[Buf("y2_d")])
                    P.barrier()
                phase_out(t0, y2_d, wo1, lambda j: h1T_in[:, j, t0:t0 + TB], lng1, lnb1, lambda j: outT[:, j, t0:t0 + TB], F32)

            for blk in range(NB):
                l1_block(blk)
            if not full1:
                P.dma("sp", xloc_out[0], car[:], reads=[car], writes=[xloc_out])
                P.dma("sp", xloc_out[1], cai[:], reads=[cai], writes=[xloc_out])
        P.barrier()
    print("instructions:", P.ninst, flush=True)
    return nc


def _chunks(w, idx_tiles):
    E = w.shape[1]
    wt = w.reshape(KT, 128, E // 128, 128)
    wt = wt[:, :, idx_tiles, :]
    return np.ascontiguousarray(wt.transpose(2, 1, 0, 3))


def _scan_layout(a):
    return np.ascontiguousarray(a.reshape(128, 2, 64).transpose(1, 2, 0).reshape(128, 128))


def _prep_s5(inp):
    lam = np.stack([_scan_layout(inp["od_lam_re"][0]), _scan_layout(inp["od_lam_im"][0]),
                    _scan_layout(np.broadcast_to(inp["od_log_step"][0][:, None], (256, 64)))]).astype(np.float32)
    Bh = np.zeros((2, 128, 32, 128), np.float32)
    Ch = np.zeros((2, 128, 128, 32), np.float32)
    for r, (bk, ck) in enumerate([("od_b_re", "od_c_re"), ("od_b_im", "od_c_im")]):
        b = inp[bk][0]
        c = inp[ck][0]
        for gq in range(256):
            i, gl = gq // 8, gq % 8
            g2 = gl % 2
            Bh[r, gl * 16:(gl + 1) * 16, i, g2 * 64:(g2 + 1) * 64] = b[gq].T
            Ch[r, g2 * 64:(g2 + 1) * 64, gq // 2, g2 * 16:(g2 + 1) * 16] = c[gq].T
    return lam, Bh, Ch


def run_pipeline(inp, x, NT):
    ids = list(range(NCORES))
    prm = pack_params(inp)
    xTs, xhTs = [], []
    for c in range(NCORES):
        b, sg = c // 4, c % 4
        xc = x[b, sg * NT:(sg + 1) * NT]
        xTs.append(np.ascontiguousarray(xc.reshape(NT, 32, 128).transpose(2, 1, 0)))
        xh = np.zeros((2, D), np.float32) if sg == 0 else x[b, sg * NT - 2:sg * NT]
        xhTs.append(np.ascontiguousarray(xh.reshape(2, 32, 128).transpose(2, 1, 0)))
    w0 = _chunks(inp["ev_w_in"][0], list(range(128)))
    ncA = build(NT, "A")
    wA = np.ascontiguousarray(w0[64:96])
    rA = run_bass_kernel_spmd(ncA, [{"prm": prm, "xT": xTs[c], "w0": wA} for c in ids], core_ids=ids).results
    wo0 = _chunks(inp["ev_w_out"][0], list(range(32)))
    w1 = _chunks(inp["od_w_in"][0], list(range(64)))
    lam, Bh, Ch = _prep_s5(inp)
    mapsB = []
    for c in ids:
        sg = c % 4
        slS = np.zeros((3, 128, 16, 128), np.float32)
        slD = np.zeros((3, 128, 16), np.float32)
        for j in range(sg):
            slS[3 - sg + j] = rA[c - sg + j]["sloc"]
            slD[3 - sg + j] = rA[c - sg + j]["dtot"]
        mapsB.append({"prm": prm, "xT": xTs[c], "xhT": xhTs[c], "w0": w0, "wo0": wo0, "sl_S": slS, "sl_D": slD,
                      "w1": np.ascontiguousarray(w1[:32]), "lamT": lam, "Bh": Bh, "Ch": Ch})
    ncB = build(NT, "B")
    rB = run_bass_kernel_spmd(ncB, mapsB, core_ids=ids).results
    del w0, wo0, mapsB
    wg1 = _chunks(inp["od_w_glu"][0], list(range(32)))
    wo1 = _chunks(inp["od_w_out"][0], list(range(32)))
    mapsC = []
    for c in ids:
        sg = c % 4
        xs = np.zeros((3, 2, 128, 128), np.float32)
        for j in range(sg):
            xs[3 - sg + j] = rB[c - sg + j]["xloc"]
        mapsC.append({"prm": prm, "h1T": rB[c]["h1T"], "w1": w1, "wg1": wg1, "wo1": wo1, "xs": xs, "lamT": lam, "Bh": Bh, "Ch": Ch})
    ncC = build(NT, "C")
    rC = run_bass_kernel_spmd(ncC, mapsC, core_ids=ids).results
    out = np.zeros((2, 4 * NT, D), np.float32)
    for c in ids:
        b, sg = c // 4, c % 4
        out[b, sg * NT:(sg + 1) * NT] = rC[c]["outT"].transpose(2, 1, 0).reshape(NT, D)
    return out, rB


def kernel(**inp):
    inp = {k: np.asarray(v) for k, v in inp.items()}
    out, _ = run_pipeline(inp, inp["x"], 2048)
    return out


def pack_params(inp):
    prm = np.zeros((3, 128, 128), np.float32)
    prm[0, 0:48] = inp["hg_lb_logits"].reshape(48, 128)
    prm[0, 48:96] = inp["ev_conv_w"][0].reshape(48, 128)
    prm[0, 96:128] = inp["ev_ln_g"][0].reshape(32, 128)
    prm[1, 0:32] = inp["ev_ln_b"][0].reshape(32, 128)
    prm[1, 32:64] = inp["od_ln_g"][0].reshape(32, 128)
    prm[1, 64:96] = inp["od_ln_b"][0].reshape(32, 128)
    prm[1, 96:128] = inp["od_d"][0].reshape(32, 128)
    prm[2, 0:32] = inp["od_b_glu"][0].reshape(32, 128)
    prm[2, 32] = inp["ev_hg_norm"][0]
    return prm
```
